# Optimizing a Trainium2 kernel written in Bass

```python
import jax, jax.numpy as jnp
from jax import lax
import numpy as np

D_MODEL = 1024
BATCH = 8
SEQ = 2048
DEPTH = 2

MEM_LEN = 256
XA_HEADS = 4
XA_HEAD_DIM = D_MODEL // XA_HEADS
D_FF = 2816
FOURIER_WIDTH = D_MODEL // 2
FOURIER_GROUPS = 4
FOURIER_GC = FOURIER_WIDTH // FOURIER_GROUPS
CONV_WIDTH = D_MODEL // 2
CONV_GROUPS = 4
CONV_GC = CONV_WIDTH // CONV_GROUPS
CONFORMER_K = 31
EVEN_IN = FOURIER_WIDTH + 2 * CONV_WIDTH
EVEN_MIX = FOURIER_WIDTH + CONV_WIDTH
SHORT_WIDTH = D_MODEL
SHORT_K = 3
N_EVEN = (DEPTH + 1) // 2
N_ODD = DEPTH // 2
N_NORMS = 8
EPS = 1e-6
HALF = 0.5

kernel_name = "hybrid_fourier_conformer_shortconv_encoder"


def rms_norm(x, g):
    xf = x.astype(jnp.float32)
    y = xf * lax.rsqrt(jnp.mean(xf * xf, axis=-1, keepdims=True) + EPS)
    return (y * g.astype(jnp.float32)).astype(x.dtype)


def swiglu(x, w_gate, w_up, w_down):
    return (jax.nn.silu(x @ w_gate) * (x @ w_up)) @ w_down


def depthwise_conv(u, w):
    k = w.shape[0]
    pad = (k - 1) // 2
    return lax.conv_general_dilated(
        u, w[:, None, :].astype(u.dtype), window_strides=(1,),
        padding=((pad, k - 1 - pad),),
        dimension_numbers=("NWC", "WIO", "NWC"),
        feature_group_count=u.shape[-1])


def fourier_mix(u, w):
    b, s, _ = u.shape
    ug = u.reshape(b, s, FOURIER_GROUPS, FOURIER_GC).astype(jnp.float32)
    f = jnp.fft.fft2(ug, axes=(1, 3), norm="ortho").real.astype(u.dtype)
    y = jnp.einsum("bsgc,gcd->bsgd", f, w)
    return y.reshape(b, s, FOURIER_WIDTH)


def group_layer_norm(u, g, beta):
    b, s, _ = u.shape
    uf = u.reshape(b, s, CONV_GROUPS, CONV_GC).astype(jnp.float32)
    mu = jnp.mean(uf, axis=-1, keepdims=True)
    var = jnp.mean(jnp.square(uf - mu), axis=-1, keepdims=True)
    y = ((uf - mu) * lax.rsqrt(var + EPS)).reshape(b, s, CONV_WIDTH)
    return (y * g.astype(jnp.float32) + beta.astype(jnp.float32)).astype(u.dtype)


def even_mixer(h, w_in, fourier_w, dw_w, dw_b, gn_g, gn_b, w_out):
    z = h @ w_in
    u_f = z[..., :FOURIER_WIDTH]
    u_val = z[..., FOURIER_WIDTH:FOURIER_WIDTH + CONV_WIDTH]
    u_gate = z[..., FOURIER_WIDTH + CONV_WIDTH:]
    y_a = fourier_mix(u_f, fourier_w)
    c = u_val * jax.nn.sigmoid(u_gate)
    c = depthwise_conv(c, dw_w) + dw_b
    y_b = jax.nn.silu(group_layer_norm(c, gn_g, gn_b))
    return jnp.concatenate([y_a, y_b], axis=-1) @ w_out


def odd_mixer(h, w_in, conv_w, w_out):
    z = h @ w_in
    b_gate, c_gate, v = jnp.split(z, 3, axis=-1)
    y = b_gate * depthwise_conv(c_gate * v, conv_w)
    return y @ w_out


def cross_attention(h, m, wq, wkv, wo):
    b, s, _ = h.shape
    q = (h @ wq).reshape(b, s, XA_HEADS, XA_HEAD_DIM)
    kv = m @ wkv
    k, v = jnp.split(kv, 2, axis=-1)
    k = k.reshape(b, MEM_LEN, XA_HEADS, XA_HEAD_DIM)
    v = v.reshape(b, MEM_LEN, XA_HEADS, XA_HEAD_DIM)
    scores = jnp.einsum("bshd,bmhd->bhsm", q.astype(jnp.float32),
                        k.astype(jnp.float32)) * (XA_HEAD_DIM ** -0.5)
    p = jax.nn.softmax(scores, axis=-1).astype(h.dtype)
    o = jnp.einsum("bhsm,bmhd->bshd", p, v).reshape(b, s, D_MODEL)
    return o @ wo


def setup_inputs(seed: int = 0) -> dict:
    key = jax.random.key(seed)
    ks = jax.random.split(key, 24)
    f32 = jnp.float32

    def nrm(k, shape, fan_in):
        return jax.random.normal(k, shape, f32) * (fan_in ** -0.5)

    def gain(k, shape):
        return 1.0 + 0.05 * jax.random.normal(k, shape, f32)

    def bias(k, shape):
        return 0.02 * jax.random.normal(k, shape, f32)

    return {
        "x": jax.random.normal(ks[0], (BATCH, SEQ, D_MODEL), f32),
        "mem": jax.random.normal(ks[1], (BATCH, MEM_LEN, D_MODEL), f32),
        "norm_g": gain(ks[2], (DEPTH, N_NORMS, D_MODEL)),
        "mem_norm_g": gain(ks[3], (DEPTH, D_MODEL)),
        "ffn_w_gate": nrm(ks[4], (DEPTH, 2, D_MODEL, D_FF), D_MODEL),
        "ffn_w_up": nrm(ks[5], (DEPTH, 2, D_MODEL, D_FF), D_MODEL),
        "ffn_w_down": nrm(ks[6], (DEPTH, 2, D_FF, D_MODEL), D_FF),
        "xa_wq": nrm(ks[7], (DEPTH, D_MODEL, D_MODEL), D_MODEL),
        "xa_wkv": nrm(ks[8], (DEPTH, D_MODEL, 2 * D_MODEL), D_MODEL),
        "xa_wo": nrm(ks[9], (DEPTH, D_MODEL, D_MODEL), D_MODEL),
        "ev_w_in": nrm(ks[10], (N_EVEN, D_MODEL, EVEN_IN), D_MODEL),
        "ev_fourier_w": nrm(ks[11], (N_EVEN, FOURIER_GROUPS, FOURIER_GC, FOURIER_GC), FOURIER_GC),
        "ev_dw_w": nrm(ks[12], (N_EVEN, CONFORMER_K, CONV_WIDTH), CONFORMER_K),
        "ev_dw_b": bias(ks[13], (N_EVEN, CONV_WIDTH)),
        "ev_gn_g": gain(ks[14], (N_EVEN, CONV_WIDTH)),
        "ev_gn_b": bias(ks[15], (N_EVEN, CONV_WIDTH)),
        "ev_w_out": nrm(ks[16], (N_EVEN, EVEN_MIX, D_MODEL), EVEN_MIX),
        "od_w_in": nrm(ks[17], (N_ODD, D_MODEL, 3 * SHORT_WIDTH), D_MODEL),
        "od_conv_w": nrm(ks[18], (N_ODD, SHORT_K, SHORT_WIDTH), SHORT_K),
        "od_w_out": nrm(ks[19], (N_ODD, SHORT_WIDTH, D_MODEL), SHORT_WIDTH),
    }


def reference(x, mem, norm_g, mem_norm_g, ffn_w_gate, ffn_w_up, ffn_w_down,
              xa_wq, xa_wkv, xa_wo, ev_w_in, ev_fourier_w, ev_dw_w, ev_dw_b,
              ev_gn_g, ev_gn_b, ev_w_out, od_w_in, od_conv_w, od_w_out):
    for l in range(DEPTH):
        g = norm_g[l]
        h = swiglu(rms_norm(x, g[0]), ffn_w_gate[l, 0], ffn_w_up[l, 0], ffn_w_down[l, 0])
        x = x + HALF * rms_norm(h, g[1])
        h = rms_norm(x, g[2])
        i = l // 2
        if l % 2 == 0:
            h = even_mixer(h, ev_w_in[i], ev_fourier_w[i], ev_dw_w[i], ev_dw_b[i],
                           ev_gn_g[i], ev_gn_b[i], ev_w_out[i])
        else:
            h = odd_mixer(h, od_w_in[i], od_conv_w[i], od_w_out[i])
        x = x + rms_norm(h, g[3])
        m = rms_norm(mem, mem_norm_g[l])
        h = cross_attention(rms_norm(x, g[4]), m, xa_wq[l], xa_wkv[l], xa_wo[l])
        x = x + rms_norm(h, g[5])
        h = swiglu(rms_norm(x, g[6]), ffn_w_gate[l, 1], ffn_w_up[l, 1], ffn_w_down[l, 1])
        x = x + HALF * rms_norm(h, g[7])
    return x
```

```python
import numpy as np
import concourse.bass as bass
import concourse.mybir as mybir
from concourse.bass_utils import run_bass_kernel_spmd

F32 = mybir.dt.float32
BF16 = mybir.dt.bfloat16
AF = mybir.ActivationFunctionType
ALU = mybir.AluOpType

D = 1024
S = 2048
DFF = 2816
NJ = DFF // 128
MEM = 256
L = 2
TB = 512
NTB = 4
EPS = 1e-6
NSLOT = 8
SLOT_ELEMS = 1024
NCORES = 8

SB_BASE = 16512
SB_TOP = 229344
OFF_X = 0
OFF_ARENA = OFF_X + 65536
ARENA_BYTES = 110592
OFF_SLOTS = OFF_ARENA + ARENA_BYTES
OFF_SQ = OFF_SLOTS + NSLOT * 2048
OFF_R = OFF_SQ + 7 * 1024
OFF_S = OFF_R + 2 * 2048
OFF_CF = OFF_S + 2 * 2048
NCF = 128 + 16 + 124 + 4 + 4 + 4 + 24
CF_G, CF_MG, CF_DW, CF_DWB, CF_GNG, CF_GNB, CF_OD = 0, 128, 144, 268, 272, 276, 280
OFF_CB = OFF_CF + NCF * 4
NCB = 512 + 256 + 128
CB_FW, CB_CSC, CB_ID = 0, 512, 768
OFF_ONES = OFF_CB + NCB * 2
OFF_END = OFF_ONES + 256 + 256 + 512
assert SB_BASE + OFF_END <= SB_TOP, (SB_BASE + OFF_END, SB_TOP)

A_XN = 0
A_H = 32768
A_HO = 77824


def _pkc(mat):
    nk = mat.shape[0] // 128
    return np.ascontiguousarray(mat.reshape(nk, 128, mat.shape[1]).transpose(1, 0, 2)).reshape(128, -1)


def _dft_tables():
    n = np.arange(S, dtype=np.int64)
    m = (n[:, None] * n[None, :]) % S
    ang = 2.0 * np.pi * m.astype(np.float64) / S
    cs = (np.cos(ang) / np.sqrt(S)).astype(np.float32)
    ns = (-np.sin(ang) / np.sqrt(S)).astype(np.float32)
    return cs, ns


def panel_specs():
    specs = []
    for l in range(L):
        for f in range(2):
            for j in range(NJ):
                for nm, gu in (("ffn_w_gate", "g"), ("ffn_w_up", "u")):
                    specs.append((("gu", l, f, j, gu), 1024,
                                  lambda inp, c, nm=nm, l=l, f=f, j=j: _pkc(inp[nm][l, f][:, j * 128:(j + 1) * 128])))
            for c in range(8):
                for pc, (r0, r1) in enumerate(((0, 1024), (1024, 2048), (2048, 2816))):
                    specs.append((("d", l, f, c, pc), (r1 - r0),
                                  lambda inp, cc, l=l, f=f, c=c, r0=r0, r1=r1:
                                  _pkc(inp["ffn_w_down"][l, f][r0:r1, c * 128:(c + 1) * 128])))
        for nm, ncol in (("xa_wq", 8), ("xa_wkv", 8), ("xa_wo", 8)):
            for c in range(ncol):
                specs.append((("lin", nm, l, c), 1024,
                              lambda inp, cc, nm=nm, l=l, c=c: _pkc(inp[nm][l][:, c * 128:(c + 1) * 128])))
        for cb in range(4):
            for kh in range(2):
                specs.append((("wv", l, cb, kh), 1024,
                              lambda inp, cc, l=l, cb=cb, kh=kh:
                              _pkc(inp["xa_wkv"][l][kh * 512:(kh + 1) * 512, 1024 + cb * 256:1024 + (cb + 1) * 256])))
    for nm, ncol in (("ev_w_in", 12), ("ev_w_out", 8), ("od_w_in", 24), ("od_w_out", 8)):
        for c in range(ncol):
            specs.append((("lin", nm, 0, c), 1024,
                          lambda inp, cc, nm=nm, c=c: _pkc(inp[nm][0][:, c * 128:(c + 1) * 128])))
    for which in range(2):
        for kb in range(4):
            for ttp in range(4):
                specs.append((("dft", which, kb, ttp), 1024,
                              lambda inp, cc, which=which, kb=kb, ttp=ttp:
                              _pkc(cc["dft"][which][ttp * 256:(ttp + 1) * 256, kb * 512:(kb + 1) * 512])))
    return specs


_SPECS = panel_specs()
PANEL = {}
_off = 0
for _k, _n, _g in _SPECS:
    PANEL[_k] = (_off, _n)
    _off += 128 * _n
WFLAT_ELEMS = _off


def pack_weights(inp):
    cache = {"dft": _dft_tables()}
    flat = np.empty(WFLAT_ELEMS, dtype=np.float32)
    for key, n, g in _SPECS:
        off, _ = PANEL[key]
        a = g(inp, cache)
        assert a.shape == (128, n), (key, a.shape)
        flat[off:off + 128 * n] = a.reshape(-1)
    return flat


def pack_consts(inp):
    cf = np.zeros((128, NCF), np.float32)
    ng = inp["norm_g"]
    for l in range(L):
        for n in range(8):
            for c in range(8):
                cf[:, CF_G + (l * 8 + n) * 8 + c] = ng[l, n, c * 128:(c + 1) * 128]
        for c in range(8):
            cf[:, CF_MG + l * 8 + c] = inp["mem_norm_g"][l, c * 128:(c + 1) * 128]
    for i in range(4):
        for j in range(31):
            cf[:, CF_DW + i * 31 + j] = inp["ev_dw_w"][0, j, i * 128:(i + 1) * 128]
        cf[:, CF_DWB + i] = inp["ev_dw_b"][0, i * 128:(i + 1) * 128]
        cf[:, CF_GNG + i] = inp["ev_gn_g"][0, i * 128:(i + 1) * 128]
        cf[:, CF_GNB + i] = inp["ev_gn_b"][0, i * 128:(i + 1) * 128]
    for i in range(8):
        for j in range(3):
            cf[:, CF_OD + i * 3 + j] = inp["od_conv_w"][0, j, i * 128:(i + 1) * 128]
    cb = np.zeros((128, NCB), np.float32)
    for g in range(4):
        cb[:, CB_FW + g * 128:CB_FW + (g + 1) * 128] = inp["ev_fourier_w"][0, g]
    c = np.arange(128)
    ang = 2.0 * np.pi * ((c[:, None] * c[None, :]) % 128) / 128.0
    cb[:, CB_CSC:CB_CSC + 128] = np.cos(ang) / np.sqrt(128.0)
    cb[:, CB_CSC + 128:CB_CSC + 256] = np.sin(ang) / np.sqrt(128.0)
    cb[:, CB_ID:CB_ID + 128] = np.eye(128)
    return cf, cb


class K:
    ENGS = ("pe", "act", "dve", "pool", "sp")

    def __init__(self, sched=None):
        self.q = {e: [] for e in self.ENGS}
        self.cnt = {}
        self.seen = {}
        self.tr = {}
        self.deferred = []
        self._in_def = False
        self._bank = 0
        self.reserved = set()
        self._rot = {}
        self.load = {"act": 0, "dve": 0}
        self.ws = WS(self, sched)

    def _collect(self, eng, reads, writes):
        need = {}
        for k_ in reads:
            e = self.tr.get(k_)
            if e and e[0]:
                s, v = e[0]
                if need.get(s, 0) < v:
                    need[s] = v
        for k_ in writes:
            e = self.tr.get(k_)
            if e:
                if e[0]:
                    s, v = e[0]
                    if need.get(s, 0) < v:
                        need[s] = v
                for s, v in e[1].items():
                    if need.get(s, 0) < v:
                        need[s] = v
        out = []
        for s, v in need.items():
            if eng == "pe" and s == "pe":
                continue
            if self.seen.get((eng, s), 0) >= v:
                continue
            self.seen[(eng, s)] = v
            out.append((s, v))
        return out

    def _commit(self, sem, val, reads, writes):
        for k_ in writes:
            self.tr[k_] = [(sem, val), {}]
        for k_ in reads:
            e = self.tr.get(k_)
            if e is None:
                e = self.tr[k_] = [None, {}]
            if e[1].get(sem, 0) < val:
                e[1][sem] = val

    def op(self, eng, fn, reads=(), writes=()):
        self._check(reads, writes)
        waits = self._collect(eng, reads, writes)
        self.cnt[eng] = self.cnt.get(eng, 0) + 1
        self.q[eng].append((waits, fn, (eng, 1)))
        self._commit(eng, self.cnt[eng], reads, writes)

    def dma(self, eng, fn, sem, reads=(), writes=()):
        self._check(reads, writes)
        waits = self._collect(eng, reads, writes)
        self.cnt[sem] = self.cnt.get(sem, 0) + 16
        self.q[eng].append((waits, fn, (sem, 16)))
        self._commit(sem, self.cnt[sem], reads, writes)

    def wait_all(self, eng, sems):
        self.q[eng].append(([(s, self.cnt[s]) for s in sems if self.cnt.get(s, 0) > 0], None, None))

    def mm(self, out, pairs, bank, start=True, stop=True):
        n = len(pairs)
        allreads = []
        bk = ("P", bank)
        self._check([k_ for p_ in pairs for k_ in p_[2]], [bk])
        for i, (l_, r_, rd) in enumerate(pairs):
            last = i == n - 1
            waits = self._collect("pe", rd, [bk] if (i == 0 and start) else [])
            st = bool(start and i == 0)
            sp = bool(stop and last)
            fn = (lambda e, l_=l_, r_=r_, st=st, sp=sp: e.matmul(out, lhsT=l_, rhs=r_, start=st, stop=sp))
            inc = None
            if last:
                self.cnt["pe"] = self.cnt.get("pe", 0) + 1
                inc = ("pe", 1)
            self.q["pe"].append((waits, fn, inc))
            allreads.extend(rd)
        self._commit("pe", self.cnt["pe"], allreads, [bk])
        self._tick()

    def defer(self, n, tag, fn, rkeys=(), wkeys=()):
        self.deferred.append([n, tag, fn, set(rkeys) | set(wkeys), set(wkeys)])

    def _check(self, reads, writes):
        if self._in_def:
            return
        while self.deferred:
            hit = -1
            for i, d in enumerate(self.deferred):
                rw, w = d[3], d[4]
                if not rw:
                    continue
                if any(k_ in rw for k_ in writes) or any(k_ in w for k_ in reads):
                    hit = i
            if hit < 0:
                return
            for _ in range(hit + 1):
                self._run_head()

    def _run_head(self):
        n, tag, fn, _rw, _w = self.deferred.pop(0)
        prev = self._in_def
        self._in_def = True
        fn()
        self._in_def = prev

    def _tick(self):
        if self._in_def:
            return
        for d in self.deferred:
            d[0] -= 1
        while self.deferred and self.deferred[0][0] <= 0:
            self._run_head()

    def need(self, tag):
        idx = [i for i, d in enumerate(self.deferred) if d[1] == tag]
        if not idx:
            return
        last = idx[-1]
        target = self.deferred[last]
        while any(d is target for d in self.deferred):
            self._run_head()

    def flush(self):
        while self.deferred:
            self._run_head()

    def bank(self, reserve=False):
        for _ in range(8):
            b = self._bank
            self._bank = (self._bank + 1) % 8
            if b not in self.reserved:
                if reserve:
                    self.reserved.add(b)
                return b
        raise RuntimeError("no free PSUM bank")

    def rot(self, name, n):
        i = self._rot.get(name, 0)
        self._rot[name] = (i + 1) % n
        return i

    def evac_eng(self, elems):
        e = "act" if self.load["act"] <= self.load["dve"] else "dve"
        self.load[e] += elems
        return e


class Slot:
    def __init__(self, pos, idx, t):
        self.pos, self.idx, self.t = pos, idx, t
        self.key = ("W", idx)

    def w(self, kk):
        return self.t[:, kk * 128:(kk + 1) * 128]


class WS:
    def __init__(self, k, sched):
        self.k, self.sched = k, sched
        self.log = []
        self.pos = 0
        self.issued = 0
        self.released = 0
        self.slots = None
        self.wflat = None

    def _issue(self):
        i = self.issued
        idx = i % NSLOT
        off, n = PANEL[self.sched[i]]
        src = self.wflat[off:off + 128 * n].rearrange("(p n) -> p n", p=128)
        dst = self.slots[idx][:, 0:n]
        self.k.dma("pool", lambda e, dst=dst, src=src: e.dma_start(out=dst, in_=src), "w%d" % idx,
                   reads=([("X", c, 0) for c in range(4)] if i == 0 else ()), writes=[("W", idx)])
        self.issued += 1

    def acquire(self, pkey):
        self.log.append(pkey)
        pos = self.pos
        self.pos += 1
        if self.sched is None:
            return Slot(pos, pos % NSLOT, self.slots[pos % NSLOT])
        assert self.sched[pos] == pkey, (pos, self.sched[pos], pkey)
        while self.issued < min(len(self.sched), max(NSLOT, pos + 1)) and self.issued < self.released + NSLOT:
            self._issue()
        assert self.issued > pos
        return Slot(pos, pos % NSLOT, self.slots[pos % NSLOT])

    def release(self, slot):
        assert slot.pos == self.released, (slot.pos, self.released)
        self.released += 1
        if self.sched is None:
            return
        while self.issued < len(self.sched) and self.issued < self.released + NSLOT:
            self._issue()


class AT:
    def __init__(self, nc, name, off, shape, dt):
        self.esz = 2 if dt == BF16 else 4
        self.off = off
        self.shape = list(shape)
        nbytes = int(np.prod(shape)) * self.esz
        assert off + nbytes <= ARENA_BYTES, (name, off, nbytes)
        self.h = nc.alloc_sbuf_tensor_at(name, [128] + list(shape), dt, offset=SB_BASE + OFF_ARENA + off)
        self.strides = [int(np.prod(shape[i + 1:])) for i in range(len(shape))]

    def __call__(self, *idx):
        assert len(idx) == len(self.shape)
        sl = [slice(None)]
        for i in idx:
            sl.append(i if isinstance(i, int) else slice(i[0], i[1]))
        ap = self.h[tuple(sl)]
        lead = idx[:-1]
        last = idx[-1]
        lo, hi = (last, last + 1) if isinstance(last, int) else last
        combos = [0]
        for d, i in enumerate(lead):
            rng = [i] if isinstance(i, int) else range(i[0], i[1])
            combos = [c + r * self.strides[d] for c in combos for r in rng]
        keys = set()
        for c in combos:
            b0 = (self.off + (c + lo) * self.esz) // 1024
            b1 = (self.off + (c + hi) * self.esz - 1) // 1024
            for b in range(b0, b1 + 1):
                keys.add(("A", b))
        return ap, list(keys)


def build(nsub=8):
    nc = bass.Bass("TRN2", target_bir_lowering=False)
    xT = nc.dram_tensor("xT", [D, S], F32, kind="ExternalInput").ap()
    memT = nc.dram_tensor("memT", [D, MEM], F32, kind="ExternalInput").ap()
    wflat = nc.dram_tensor("wflat", [WFLAT_ELEMS], F32, kind="ExternalInput").ap()
    cfd = nc.dram_tensor("cf", [128, NCF], F32, kind="ExternalInput").ap()
    cbd = nc.dram_tensor("cb", [128, NCB], F32, kind="ExternalInput").ap()
    altd = nc.dram_tensor("alt", [1, S], F32, kind="ExternalInput").ap()
    yT = nc.dram_tensor("yT", [D, S], F32, kind="ExternalOutput").ap()

    def sb(name, off, shape, dt):
        return nc.alloc_sbuf_tensor_at(name, [128] + list(shape), dt, offset=SB_BASE + off)

    X = sb("X", OFF_X, [8, S], F32)
    SLOTS = [sb("slot%d" % i, OFF_SLOTS + i * 2048, [SLOT_ELEMS], BF16) for i in range(NSLOT)]
    SQD = [sb("sqd%d" % i, OFF_SQ + i * 1024, [TB], BF16) for i in range(3)]
    SQP = sb("sqp", OFF_SQ + 3 * 1024, [4, TB], BF16)
    R = [sb("r%d" % i, OFF_R + i * 2048, [TB], F32) for i in range(2)]
    SS = [sb("s%d" % i, OFF_S + i * 2048, [TB], F32) for i in range(2)]
    CF = sb("cf_sb", OFF_CF, [NCF], F32)
    CB = sb("cb_sb", OFF_CB, [NCB], BF16)
    ONESM = sb("onesm", OFF_ONES, [128], BF16)
    ONES1 = sb("ones1", OFF_ONES + 256, [128], BF16)
    ONESF = sb("onesf", OFF_ONES + 512, [128], F32)
    P = [nc.alloc_psum_tensor("ps%d" % i, [128, TB], F32) for i in range(8)]

    XN = AT(nc, "XN", A_XN, [8, S], BF16)
    H = AT(nc, "H", A_H, [NJ, 1024], BF16)
    HO = AT(nc, "HO", A_HO, [8, 1024], F32)
    UF = AT(nc, "UF", 32768, [4, S], BF16)
    YA = AT(nc, "YA", 32768, [4, S], BF16)
    CC = AT(nc, "CC", 49152, [4, S + 32], BF16)
    FB = AT(nc, "FB", 49152 + 16640, [4, TB], BF16)
    AB = AT(nc, "AB", 0, [9, 4, 256], BF16)
    USY = AT(nc, "USY", A_HO, [4, 1152], BF16)
    UAS = AT(nc, "UAS", A_HO + 9216, [4, 1024], BF16)
    ALT = AT(nc, "ALT", A_HO + 17408, [S], BF16)
    YB = AT(nc, "YB", 0, [4, S], BF16)
    DG = [AT(nc, "DG0", 69888, [31, 128], BF16), AT(nc, "DG1", A_HO + 24576, [31, 128], BF16)]
    GS = {nm: [AT(nc, "gs_%s%d" % (nm, i), A_HO + (j * 3 + i) * 2048, [TB], F32) for i in range(3)]
          for j, nm in enumerate(("c1", "d", "d2", "sd"))}
    GSB = {"c1b": [AT(nc, "gsb_c1b%d" % i, A_HO + (3 * 3 + i) * 2048, [TB], BF16) for i in range(3)],
           "d2b": [AT(nc, "gsb_d2b%d" % i, A_HO + (2 * 3 + i) * 2048, [TB], BF16) for i in range(3)]}
    YO = AT(nc, "YO", 32768, [8, S], BF16)
    CV = [AT(nc, "CV%d" % i, 65536 + i * 8448, [S + 64], F32) for i in range(2)]
    TT = [AT(nc, "TT%d" % i, 65536 + 2 * 8448 + i * 8192, [S], F32) for i in range(2)]
    O = AT(nc, "O", 0, [8, S], BF16)
    HO2 = AT(nc, "HO2", 32768, [8, 1024], F32)
    Q = AT(nc, "Q", 32768, [8, S], BF16)
    M32 = AT(nc, "M32", 65536, [8, MEM], F32)
    MN = AT(nc, "MN", 73728, [8, MEM], BF16)
    KT = AT(nc, "KT", 65536, [8, MEM], BF16)
    V = AT(nc, "V", 69632, [2, D], BF16)
    ET = [AT(nc, "ET%d" % i, 86016 + i * 2048, [2, TB], BF16) for i in range(3)]
    RD = [AT(nc, "RD%d" % i, 92160 + i * 2048, [TB], F32) for i in range(2)]

    def tbr(tb):
        return (tb * TB, (tb + 1) * TB)

    def G(l, n, c):
        i = CF_G + (l * 8 + n) * 8 + c
        return CF[:, i:i + 1]

    def program(k):
        ws = k.ws
        ws.slots = SLOTS
        ws.wflat = wflat
        xk = lambda c, tb: ("X", c, tb)

        k.dma("sp", lambda e: e.dma_start(out=CF[:, :], in_=cfd), "c0", writes=[("CF",)])
        k.dma("pool", lambda e: e.dma_start(out=CB[:, :], in_=cbd), "c1", writes=[("CB",)])
        xv = xT.rearrange("(c p) t -> p c t", p=128)
        for tb in range(NTB):
            lo, hi = tbr(tb)
            for ch in range(2):
                k.dma("sp", lambda e, lo=lo, hi=hi, ch=ch: e.dma_start(out=X[:, 4 * ch:4 * ch + 4, lo:hi],
                                                                       in_=xv[:, 4 * ch:4 * ch + 4, lo:hi]), "xl%d_%d" % (tb, ch),
                      writes=[xk(c, tb) for c in range(4 * ch, 4 * ch + 4)])
        k.op("dve", lambda e: e.memset(ONESM[:, :], 1.0 / 1024.0), writes=[("ONES",)])
        k.op("dve", lambda e: e.memset(ONES1[:, :], 1.0), writes=[("ONES",)])
        k.op("dve", lambda e: e.memset(ONESF[:, :], 1.0 / 128.0), writes=[("ONES",)])
        CONST = [("CF",), ("CB",), ("ONES",)]

        def rstd_from(bank, n, sc):
            r = k.rot("R", 2)
            k.op("act", lambda e: e.activation(out=R[r][:, 0:n], in_=P[bank][:, 0:n], func=AF.Ln, bias=sc * EPS, scale=sc),
                 reads=[("P", bank)], writes=[("R", r)])
            k.op("act", lambda e: e.activation(out=R[r][:, 0:n], in_=R[r][:, 0:n], func=AF.Exp, scale=-0.5),
                 reads=[("R", r)], writes=[("R", r)])
            k.reserved.discard(bank)
            return r

        def chain_keys(tbs, with_ho):
            xks = [xk(c, tb) for c in range(8) for tb in tbs]
            xnk = []
            for tb in tbs:
                xnk += XN((0, 8), tbr(tb))[1]
            hok = HO((0, 8), (0, 1024))[1] if with_ho else []
            return hok + xks, hok + xks + xnk

        def run_seq(steps):
            while any(d[1] == ("seq",) for d in k.deferred):
                k.need(("seq",))
            n_ = len(steps)
            suf_r = [set() for _ in range(n_ + 1)]
            suf_w = [set() for _ in range(n_ + 1)]
            for i in range(n_ - 1, -1, -1):
                suf_r[i] = suf_r[i + 1] | set(steps[i][1].rk)
                suf_w[i] = suf_w[i + 1] | set(steps[i][1].wk)

            def go(i):
                if i >= n_:
                    return
                d, fn = steps[i]

                def wrapped():
                    fn()
                    go(i + 1)
                if d <= 0:
                    wrapped()
                else:
                    k.defer(d, ("seq",), wrapped, suf_r[i], suf_w[i])
            go(0)

        def keyed(fn, rk=(), wk=()):
            fn.rk = list(rk)
            fn.wk = list(wk)
            return fn

        def both(*fns):
            def fn():
                for f_ in fns:
                    f_()
            return keyed(fn, [k_ for f_ in fns for k_ in f_.rk], [k_ for f_ in fns for k_ in f_.wk])

        def prenorm_steps(l, n, tb):
            lo, hi = tbr(tb)
            st = {}

            def sa():
                k.op("act", lambda e: e.activation(out=SQP[:, :, :], in_=X[:, 0:4, lo:hi], func=AF.Square),
                     reads=[xk(c, tb) for c in range(4)], writes=[("SQP",)])

            def sb_():
                st["b"] = bnk = k.bank(reserve=True)
                for c in range(4):
                    k.mm(P[bnk][:, :], [(ONESM[:, :], SQP[:, c, :], [("SQP",), ("ONES",)])], bnk, start=(c == 0), stop=False)
                k.op("act", lambda e: e.activation(out=SQP[:, :, :], in_=X[:, 4:8, lo:hi], func=AF.Square),
                     reads=[xk(c, tb) for c in range(4, 8)], writes=[("SQP",)])

            def xn_apply(c):
                r = st["r"]
                oap, okeys = XN(c, (lo, hi))
                k.op("dve", lambda e, c=c, oap=oap: e.scalar_tensor_tensor(
                    out=oap, in0=X[:, c, lo:hi], scalar=G(l, n, c), in1=R[r][:, :], op0=ALU.mult, op1=ALU.mult),
                    reads=[xk(c, tb), ("R", r)] + CONST, writes=okeys)

            def sc1():
                bnk = st["b"]
                for c in range(4):
                    k.mm(P[bnk][:, :], [(ONESM[:, :], SQP[:, c, :], [("SQP",), ("ONES",)])], bnk, start=False, stop=(c == 3))
                st["r"] = rstd_from(bnk, TB, 1.0)
                for c in range(4):
                    xn_apply(c)

            def sc2():
                for c in range(4, 8):
                    xn_apply(c)
            keyed(sa, [xk(c, tb) for c in range(4)])
            keyed(sb_, [xk(c, tb) for c in range(4, 8)])
            keyed(sc1, [xk(c, tb) for c in range(4)], XN((0, 4), (lo, hi))[1])
            keyed(sc2, [xk(c, tb) for c in range(4, 8)], XN((4, 8), (lo, hi))[1])
            return sa, sb_, sc1, sc2

        def prenorm_chain(l, n, tbs):
            steps = []
            for tb in tbs:
                sa, sb_, sc1, sc2 = prenorm_steps(l, n, tb)
                steps += [(0 if not steps else 1, sa), (3, sb_), (3, sc1), (1, sc2)]
            run_seq(steps)

        HO_PENDING = []

        class HalfOut:
            def __init__(self, l, n_post, half, tbs, hoap=None):
                self.l, self.n, self.half, self.tbs = l, n_post, half, tbs
                self.hoap = hoap or (lambda c, tl: HO(c, tl))
                self.pst = {}
                self.pend = []

            def evac(self, c, tb, bank):
                for o_ in list(HO_PENDING):
                    if o_ is not self:
                        o_.stat_mm(0)
                if self not in HO_PENDING:
                    HO_PENDING.append(self)
                tl = ((tb % 2) * TB, (tb % 2 + 1) * TB)
                oap, okeys = self.hoap(c, tl)
                gcol = G(self.l, self.n, c)
                k.op("act", lambda e: e.activation(out=oap, in_=P[bank][:, :], func=AF.Copy, scale=gcol),
                     reads=[("P", bank)] + CONST, writes=okeys)
                i = k.rot("SQD", 3)
                k.op("act", lambda e: e.activation(out=SQD[i][:, :], in_=P[bank][:, :], func=AF.Square),
                     reads=[("P", bank)], writes=[("SQD", i)])
                self.pend.append((tb, c, i))
                self.stat_mm(2)

            def stat_mm(self, keep):
                if keep == 0 and self in HO_PENDING:
                    HO_PENDING.remove(self)
                while len(self.pend) > keep:
                    tb, c, i = self.pend.pop(0)
                    if tb not in self.pst:
                        self.pst[tb] = k.bank(reserve=True)
                    bnk = self.pst[tb]
                    k.mm(P[bnk][:, :], [(ONESM[:, :], SQD[i][:, :], [("SQD", i), ("ONES",)])], bnk,
                         start=(c == 0), stop=(c == 7))

            def finish(self, nxt, hf, cpb=1):
                fin_stats = keyed(lambda: self.stat_mm(0), [("SQD", i_) for i_ in range(3)], [])
                ta, tb_ = self.tbs[0], self.tbs[-1]
                sc = 4.0 if self.half else 1.0
                rr = {}
                if hf == 1:
                    cpb = 1

                def add_eng_for(tb):
                    if nxt is None:
                        return "dve"
                    if hf == 0:
                        return "pool"
                    return "pool" if tb == ta else "dve"

                def A(tb, c0):
                    def fn():
                        lo, hi = tbr(tb)
                        tl = ((tb % 2) * TB, (tb % 2 + 1) * TB)
                        if c0 == 0:
                            rr[tb] = rstd_from(self.pst[tb], TB, sc)
                        r = rr[tb]
                        for c in range(c0, c0 + cpb):
                            hap, hkeys = self.hoap(c, tl)
                            k.op("dve", lambda e, hap=hap: e.tensor_tensor(out=hap, in0=hap, in1=R[r][:, :], op=ALU.mult),
                                 reads=hkeys + [("R", r)], writes=hkeys)
                            k.op(add_eng_for(tb), lambda e, hap=hap, c=c: e.tensor_tensor(out=X[:, c, lo:hi], in0=X[:, c, lo:hi], in1=hap,
                                                                                         op=ALU.add),
                                 reads=hkeys + [xk(c, tb)], writes=[xk(c, tb)])
                    tl_ = ((tb % 2) * TB, (tb % 2 + 1) * TB)
                    ks = [xk(c, tb) for c in range(c0, c0 + cpb)]
                    for c in range(c0, c0 + cpb):
                        ks += self.hoap(c, tl_)[1]
                    return keyed(fn, ks, ks)
                cs = list(range(0, 8, cpb))
                steps = [(2, fin_stats)] + [(1, A(ta, c0)) for c0 in cs]
                if len(self.tbs) == 1:
                    assert nxt is None
                    steps += [(0, keyed(lambda: store_out(ta), [xk(c, ta) for c in range(8)]))]
                elif nxt is None:
                    steps += [(0, keyed(lambda: store_out(ta), [xk(c, ta) for c in range(8)]))]
                    steps += [(1, A(tb_, c0)) for c0 in cs]
                    steps += [(0, keyed(lambda: store_out(tb_), [xk(c, tb_) for c in range(8)]))]
                elif cpb == 1:
                    sa0, sb0, sc10, sc20 = prenorm_steps(nxt[0], nxt[1], ta)
                    sa1, sb1, sc11, sc21 = prenorm_steps(nxt[0], nxt[1], tb_)
                    steps += [(1, both(A(tb_, 0), sa0)), (1, A(tb_, 1)), (1, A(tb_, 2)), (1, both(A(tb_, 3), sb0)),
                              (1, A(tb_, 4)), (1, A(tb_, 5)), (1, both(A(tb_, 6), sc10)), (1, both(A(tb_, 7), sc20)),
                              (2, sa1), (3, sb1), (3, sc11), (1, sc21)]
                else:
                    assert cpb == 2
                    def A2(c0):
                        def fn():
                            for tb in (ta, tb_):
                                if c0 == 0:
                                    rr[tb] = rstd_from(self.pst[tb], TB, sc)
                            for c in (c0, c0 + 1):
                                for tb in (ta, tb_):
                                    lo, hi = tbr(tb)
                                    tl = ((tb % 2) * TB, (tb % 2 + 1) * TB)
                                    r = rr[tb]
                                    hap, hkeys = self.hoap(c, tl)
                                    k.op("dve", lambda e, hap=hap, r=r: e.tensor_tensor(out=hap, in0=hap, in1=R[r][:, :], op=ALU.mult),
                                         reads=hkeys + [("R", r)], writes=hkeys)
                                    k.op("pool" if (tb == ta and hf == 0) else "dve",
                                         lambda e, hap=hap, c=c, lo=lo, hi=hi: e.tensor_tensor(
                                             out=X[:, c, lo:hi], in0=X[:, c, lo:hi], in1=hap, op=ALU.add),
                                         reads=hkeys + [xk(c, tb)], writes=[xk(c, tb)])
                        ks = []
                        for c in (c0, c0 + 1):
                            for tb in (ta, tb_):
                                ks += [xk(c, tb)] + self.hoap(c, ((tb % 2) * TB, (tb % 2 + 1) * TB))[1]
                        return keyed(fn, ks, ks)
                    sa0, sb0, sc10, sc20 = prenorm_steps(nxt[0], nxt[1], ta)
                    sa1, sb1, sc11, sc21 = prenorm_steps(nxt[0], nxt[1], tb_)
                    steps = [(2, fin_stats), (1, A2(0)), (1, A2(2)), (1, A2(4)), (1, A2(6)),
                             (9, sa0), (11, sb0), (3, sc10), (1, both(sc20, sa1)), (3, sb1), (3, sc11), (1, sc21)]
                run_seq(steps)

        def store_out(tb):
            lo, hi = tbr(tb)
            yv = yT.rearrange("(c p) t -> p c t", p=128)
            k.dma("sp", lambda e: e.dma_start(out=yv[:, :, lo:hi], in_=X[:, :, lo:hi]), "yo",
                  reads=[("X", c, tb) for c in range(8)])

        def evac_copy(oap, okeys, bank, n=TB, eng=None):
            eng = eng or k.evac_eng(n)
            if eng == "act":
                k.op("act", lambda e: e.activation(out=oap, in_=P[bank][:, 0:n], func=AF.Copy),
                     reads=[("P", bank)], writes=okeys)
            else:
                k.op("dve", lambda e: e.tensor_copy(out=oap, in_=P[bank][:, 0:n]), reads=[("P", bank)], writes=okeys)

        def out_proj(pkeys_for_c, rhs, nk, l, n_post, half, nxt, ho2=None):
            for hf in range(2):
                tbs = (2 * hf, 2 * hf + 1)
                held = []
                ho = HalfOut(l, n_post, half, tbs, hoap=(ho2 if hf == 1 else None))

                def emit_held():
                    c_, tb_, b_ = held.pop(0)
                    ho.evac(c_, tb_, b_)
                    k.reserved.discard(b_)
                for c in range(8):
                    slots = [ws.acquire(pk) for pk in pkeys_for_c(c)]
                    for tb in tbs:
                        b = k.bank(reserve=True)
                        pairs = []
                        for kk in range(nk):
                            rap, rkeys = rhs(kk, tb)
                            sl = slots[kk // 8]
                            pairs.append((sl.w(kk % 8), rap, rkeys + [sl.key]))
                        k.mm(P[b][:, :], pairs, b)
                        held.append((c, tb, b))
                        if len(held) > 3:
                            emit_held()
                    for sl in slots:
                        ws.release(sl)
                while held:
                    emit_held()
                ho.finish(nxt, hf, cpb=2)

        def need_xn(tb):
            pass

        def ffn(l, f, n_post, nxt):
            for hf in range(2):
                tbs = (2 * hf, 2 * hf + 1)
                for tb in tbs:
                    need_xn(tb)
                for j in range(NJ):
                    wg = ws.acquire(("gu", l, f, j, "g"))
                    wu = ws.acquire(("gu", l, f, j, "u"))
                    for tb in tbs:
                        lo, hi = tbr(tb)
                        tl = ((tb % 2) * TB, (tb % 2 + 1) * TB)
                        bg = k.bank()
                        pairs = []
                        for kk in range(8):
                            rap, rkeys = XN(kk, (lo, hi))
                            pairs.append((wg.w(kk), rap, rkeys + [wg.key]))
                        k.mm(P[bg][:, :], pairs, bg)
                        s = k.rot("S", 2)
                        k.op("act", lambda e, bg=bg, s=s: e.activation(out=SS[s][:, :], in_=P[bg][:, :], func=AF.Silu),
                             reads=[("P", bg)], writes=[("S", s)])
                        bu = k.bank()
                        pairs = []
                        for kk in range(8):
                            rap, rkeys = XN(kk, (lo, hi))
                            pairs.append((wu.w(kk), rap, rkeys + [wu.key]))
                        k.mm(P[bu][:, :], pairs, bu)
                        hap, hkeys = H(j, tl)
                        k.op("dve", lambda e, bu=bu, s=s, hap=hap: e.tensor_tensor(out=hap, in0=P[bu][:, :], in1=SS[s][:, :],
                                                                                    op=ALU.mult),
                             reads=[("P", bu), ("S", s)], writes=hkeys)
                    ws.release(wg)
                    ws.release(wu)
                if nxt is None and hf == 1:
                    for tb in tbs:
                        tl = ((tb % 2) * TB, (tb % 2 + 1) * TB)
                        ho = HalfOut(l, n_post, True, (tb,))
                        for c in range(8):
                            slots = [ws.acquire(("d", l, f, c, pc)) for pc in range(3)]
                            b = k.bank()
                            pairs = []
                            for j in range(NJ):
                                rap, rkeys = H(j, tl)
                                sl = slots[j // 8]
                                pairs.append((sl.w(j % 8), rap, rkeys + [sl.key]))
                            k.mm(P[b][:, :], pairs, b)
                            ho.evac(c, tb, b)
                            for sl in slots:
                                ws.release(sl)
                        ho.finish(nxt, hf)
                    continue
                ho = HalfOut(l, n_post, True, tbs)
                for c in range(8):
                    slots = [ws.acquire(("d", l, f, c, pc)) for pc in range(3)]
                    for tb in tbs:
                        tl = ((tb % 2) * TB, (tb % 2 + 1) * TB)
                        b = k.bank()
                        pairs = []
                        for j in range(NJ):
                            rap, rkeys = H(j, tl)
                            sl = slots[j // 8]
                            pairs.append((sl.w(j % 8), rap, rkeys + [sl.key]))
                        k.mm(P[b][:, :], pairs, b)
                        ho.evac(c, tb, b)
                    for sl in slots:
                        ws.release(sl)
                ho.finish(nxt, hf)

        def even_mixer(l, n_post, nxt):
            for tb in range(NTB):
                need_xn(tb)
            idap = CB[:, CB_ID:CB_ID + 128]
            def build_dg(i):
                dg = DG[i % 2]
                for j in range(31):
                    dap, dkeys = dg(j, (0, 128))
                    col = CF[:, CF_DW + i * 31 + j:CF_DW + i * 31 + j + 1]
                    k.op("dve", lambda e, dap=dap, col=col: e.tensor_scalar(out=dap, in0=idap, scalar1=col, scalar2=None,
                                                                             op0=ALU.mult),
                         reads=CONST, writes=dkeys)
            for i in range(4):
                for (a, b_) in ((0, 15), (15 + S, S + 32)):
                    ap, keys = CC(i, (a, b_))
                    k.op("pool", lambda e, ap=ap: e.memset(ap, 0.0), writes=keys)
            for hf, i in [(h_, i_) for h_ in range(2) for i_ in range(4)]:
                wgt = ws.acquire(("lin", "ev_w_in", 0, 8 + i))
                wvl = ws.acquire(("lin", "ev_w_in", 0, 4 + i))
                for tb in (2 * hf, 2 * hf + 1):
                    lo, hi = tbr(tb)
                    bg = k.bank()
                    k.mm(P[bg][:, :], [(wgt.w(kk),) + tuple(_rk(XN(kk, (lo, hi)), wgt.key)) for kk in range(8)], bg)
                    s = k.rot("S", 2)
                    k.op("act", lambda e, bg=bg, s=s: e.activation(out=SS[s][:, :], in_=P[bg][:, :], func=AF.Sigmoid),
                         reads=[("P", bg)], writes=[("S", s)])
                    bv = k.bank()
                    k.mm(P[bv][:, :], [(wvl.w(kk),) + tuple(_rk(XN(kk, (lo, hi)), wvl.key)) for kk in range(8)], bv)
                    cap, ckeys = CC(i, (15 + lo, 15 + hi))
                    k.op("dve", lambda e, bv=bv, s=s, cap=cap: e.tensor_tensor(out=cap, in0=P[bv][:, :], in1=SS[s][:, :],
                                                                                op=ALU.mult),
                         reads=[("P", bv), ("S", s)], writes=ckeys)
                ws.release(wgt)
                ws.release(wvl)
            for hf, g in [(h_, g_) for h_ in range(2) for g_ in range(4)]:
                wf = ws.acquire(("lin", "ev_w_in", 0, g))
                for tb in (2 * hf, 2 * hf + 1):
                    lo, hi = tbr(tb)
                    b = k.bank()
                    k.mm(P[b][:, :], [(wf.w(kk),) + tuple(_rk(XN(kk, (lo, hi)), wf.key)) for kk in range(8)], b)
                    oap, okeys = UF(g, (lo, hi))
                    evac_copy(oap, okeys, b)
                ws.release(wf)
            k.dma("pool", lambda e: e.dma_start(out=ALT.h[0:1, :], in_=altd), "c1", writes=ALT((0, S))[1])
            for g in range(4):
                usy, usyk = USY(g, (1, 1024))
                uas, uask = UAS(g, (1, 1024))
                ufw, ufwk = UF(g, (1, 1024))
                ufr = UF.h[:, g, 2047:1024:-1]
                ufrk = UF(g, (1025, S))[1]
                k.op("dve", lambda e, usy=usy, ufw=ufw, ufr=ufr: e.tensor_tensor(out=usy, in0=ufw, in1=ufr, op=ALU.add),
                     reads=ufwk + ufrk, writes=usyk)
                k.op("dve", lambda e, uas=uas, ufw=ufw, ufr=ufr: e.tensor_tensor(out=uas, in0=ufw, in1=ufr, op=ALU.subtract),
                     reads=ufwk + ufrk, writes=uask)
                a0, a0k = USY(g, (0, 1))
                k.op("dve", lambda e, a0=a0, g=g: e.tensor_copy(out=a0, in_=UF.h[:, g, 0:1]), reads=UF(g, (0, 1))[1], writes=a0k)
                a1, a1k = USY(g, (1024, 1152))
                k.op("dve", lambda e, a1=a1: e.memset(a1, 0.0), writes=a1k)
                a2, a2k = USY(g, (1024, 1025))
                k.op("dve", lambda e, a2=a2, g=g: e.tensor_copy(out=a2, in_=UF.h[:, g, 1024:1025]), reads=UF(g, (1024, 1025))[1],
                     writes=a2k)
                z0, z0k = UAS(g, (0, 1))
                k.op("dve", lambda e, z0=z0: e.memset(z0, 0.0), writes=z0k)
            ccos = CB[:, CB_CSC:CB_CSC + 128]
            csin = CB[:, CB_CSC + 128:CB_CSC + 256]
            for tt in range(9):
                for g in range(4):
                    b = k.bank()
                    uap, ukeys = USY(g, (tt * 128, (tt + 1) * 128))
                    k.mm(P[b][:, 0:128], [(uap, ccos, ukeys + CONST)], b)
                    oap, okeys = AB(tt, g, (0, 128))
                    evac_copy(oap, okeys, b, n=128)
                    if tt < 8:
                        b = k.bank()
                        uap, ukeys = UAS(g, (tt * 128, (tt + 1) * 128))
                        k.mm(P[b][:, 0:128], [(uap, csin, ukeys + CONST)], b)
                        oap, okeys = AB(tt, g, (128, 256))
                        evac_copy(oap, okeys, b, n=128)
            build_dg(0)
            build_dg(1)
            for kb in range(4):
                acc = [k.bank(reserve=True) for g in range(4)]
                for ttp in range(4):
                    for which in range(2):
                        w = ws.acquire(("dft", which, kb, ttp))
                        for g in range(4):
                            pairs = []
                            for ti in range(2):
                                tt = 2 * ttp + ti
                                lap, lkeys = AB(tt, g, (which * 128, (which + 1) * 128))
                                pairs.append((lap, w.t[:, ti * TB:(ti + 1) * TB], lkeys + [w.key]))
                            k.mm(P[acc[g]][:, :], pairs, acc[g], start=(ttp == 0 and which == 0), stop=False)
                        ws.release(w)
                for g in range(4):
                    lkeys = AB(8, g, (0, 128))[1]
                    k.mm(P[acc[g]][:, :], [(AB.h[0:1, 8, g, 0:128], ALT.h[0:1, kb * TB:(kb + 1) * TB],
                                            lkeys + ALT((0, S))[1])], acc[g], start=False, stop=True)

                def step3(kb=kb, acc=acc):
                    lo, hi = tbr(kb)
                    for g in range(4):
                        fap, fkeys = FB(g, (0, TB))
                        evac_copy(fap, fkeys, acc[g])
                        k.reserved.discard(acc[g])
                    for g in range(4):
                        fap, fkeys = FB(g, (0, TB))
                        b = k.bank()
                        k.mm(P[b][:, :], [(CB[:, CB_FW + g * 128:CB_FW + (g + 1) * 128], fap, fkeys + CONST)], b)
                        oap, okeys = YA(g, (lo, hi))
                        evac_copy(oap, okeys, b)
                k.defer(6, ("f3", kb), step3)
            k.flush()
            for i in range(4):
                dg = DG[i % 2]
                for tb in range(NTB):
                    lo, hi = tbr(tb)
                    b = k.bank()
                    pairs = []
                    for j in range(31):
                        dap, dkeys = dg(j, (0, 128))
                        cap, ckeys = CC(i, (lo + j, hi + j))
                        pairs.append((dap, cap, dkeys + ckeys))
                    k.mm(P[b][:, :], pairs, b)
                    gi = k.rot("GS", 3)
                    c1, c1k = GS["c1"][gi]((0, TB))
                    dv, dvk = GS["d"][gi]((0, TB))
                    d2, d2k = GS["d2"][gi]((0, TB))
                    sd, sdk = GS["sd"][gi]((0, TB))
                    c1b, c1bk = GSB["c1b"][gi]((0, TB))
                    d2b, d2bk = GSB["d2b"][gi]((0, TB))
                    k.op("act", lambda e, b=b, c1=c1, i=i: e.activation(out=c1, in_=P[b][:, :], func=AF.Identity,
                                                                        bias=CF[:, CF_DWB + i:CF_DWB + i + 1]),
                         reads=[("P", b)] + CONST, writes=c1k)
                    k.op("act", lambda e, b=b, c1b=c1b, i=i: e.activation(out=c1b, in_=P[b][:, :], func=AF.Identity,
                                                                          bias=CF[:, CF_DWB + i:CF_DWB + i + 1]),
                         reads=[("P", b)] + CONST, writes=c1bk)

                    gkeys = c1k + dvk + d2k + sdk + YB(i, (lo, hi))[1]

                    def stage_b(i=i, lo=lo, hi=hi, c1=c1, c1k=c1k, dv=dv, dvk=dvk, d2=d2, d2k=d2k, sd=sd, sdk=sdk, gkeys=gkeys,
                                c1b=c1b, c1bk=c1bk, d2b=d2b, d2bk=d2bk):
                        b2 = k.bank()
                        k.mm(P[b2][:, :], [(ONES1[:, :], c1b, c1bk + CONST)], b2)
                        k.op("dve", lambda e: e.scalar_tensor_tensor(out=dv, in0=P[b2][:, :], scalar=-1.0 / 128.0, in1=c1,
                                                                     op0=ALU.mult, op1=ALU.add),
                             reads=c1k + [("P", b2)], writes=dvk)
                        k.op("act", lambda e: e.activation(out=d2b, in_=dv, func=AF.Square), reads=dvk, writes=d2bk)

                        def stage_c():
                            b3 = k.bank()
                            k.mm(P[b3][:, :], [(ONES1[:, :], d2b, d2bk + CONST)], b3)
                            k.op("act", lambda e: e.activation(out=sd, in_=P[b3][:, :], func=AF.Ln, bias=EPS, scale=1.0 / 128.0),
                                 reads=[("P", b3)], writes=sdk)
                            k.op("act", lambda e: e.activation(out=sd, in_=sd, func=AF.Exp, scale=-0.5), reads=sdk, writes=sdk)
                            k.op("dve", lambda e: e.scalar_tensor_tensor(
                                out=d2, in0=dv, scalar=CF[:, CF_GNG + i:CF_GNG + i + 1], in1=sd, op0=ALU.mult, op1=ALU.mult),
                                reads=dvk + sdk + CONST, writes=d2k)
                            yap, ykeys = YB(i, (lo, hi))
                            k.op("act", lambda e: e.activation(out=yap, in_=d2, func=AF.Silu,
                                                               bias=CF[:, CF_GNB + i:CF_GNB + i + 1]),
                                 reads=d2k + CONST, writes=ykeys)
                        k.defer(1, ("gln",), stage_c, gkeys, gkeys)
                    k.defer(1, ("gln",), stage_b, gkeys, gkeys)
                if i + 2 < 4:
                    build_dg(i + 2)

            def rhs(kk, tb):
                lo, hi = tbr(tb)
                return tuple_list(YA(kk, (lo, hi)) if kk < 4 else YB(kk - 4, (lo, hi)))
            out_proj(lambda c: [("lin", "ev_w_out", 0, c)], rhs, 8, l, n_post, False, nxt)

        def odd_mixer(l, n_post, nxt):
            for tb in range(NTB):
                need_xn(tb)
            for cvb in range(2):
                for (a, b_) in ((0, 1), (1 + S, S + 64)):
                    ap, keys = CV[cvb]((a, b_))
                    k.op("pool", lambda e, ap=ap: e.memset(ap, 0.0), writes=keys)

            def part_b(i, wb):
                tap, tkeys = TT[i % 2]((0, S))
                for tb in range(NTB):
                    lo, hi = tbr(tb)
                    bb = k.bank()
                    k.mm(P[bb][:, :], [(wb.w(kk),) + tuple(_rk(XN(kk, (lo, hi)), wb.key)) for kk in range(8)], bb)
                    tslice, tk = TT[i % 2]((lo, hi))
                    yap, ykeys = YO(i, (lo, hi))
                    k.op("dve", lambda e, bb=bb, tslice=tslice, yap=yap: e.tensor_tensor(out=yap, in0=P[bb][:, :], in1=tslice,
                                                                                          op=ALU.mult),
                         reads=[("P", bb)] + tk, writes=ykeys)

            pend = None
            for i in range(8):
                wc = ws.acquire(("lin", "od_w_in", 0, 8 + i))
                wv = ws.acquire(("lin", "od_w_in", 0, 16 + i))
                wb = ws.acquire(("lin", "od_w_in", 0, i))
                cv = CV[i % 2]
                for tb in range(NTB):
                    lo, hi = tbr(tb)
                    bc = k.bank()
                    k.mm(P[bc][:, :], [(wc.w(kk),) + tuple(_rk(XN(kk, (lo, hi)), wc.key)) for kk in range(8)], bc)
                    s = k.rot("S", 2)
                    k.op("act", lambda e, bc=bc, s=s: e.activation(out=SS[s][:, :], in_=P[bc][:, :], func=AF.Copy),
                         reads=[("P", bc)], writes=[("S", s)])
                    bv = k.bank()
                    k.mm(P[bv][:, :], [(wv.w(kk),) + tuple(_rk(XN(kk, (lo, hi)), wv.key)) for kk in range(8)], bv)
                    cap, ckeys = cv((1 + lo, 1 + hi))
                    k.op("dve", lambda e, bv=bv, s=s, cap=cap: e.tensor_tensor(out=cap, in0=P[bv][:, :], in1=SS[s][:, :],
                                                                                op=ALU.mult),
                         reads=[("P", bv), ("S", s)], writes=ckeys)
                tap, tkeys = TT[i % 2]((0, S))
                w0 = CF[:, CF_OD + i * 3 + 0:CF_OD + i * 3 + 1]
                w1 = CF[:, CF_OD + i * 3 + 1:CF_OD + i * 3 + 2]
                w2 = CF[:, CF_OD + i * 3 + 2:CF_OD + i * 3 + 3]
                c0, c0k = cv((0, S))
                c1_, c1k = cv((1, S + 1))
                c2, c2k = cv((2, S + 2))
                k.op("dve", lambda e, tap=tap, c0=c0, w0=w0: e.tensor_scalar(out=tap, in0=c0, scalar1=w0, scalar2=None,
                                                                             op0=ALU.mult),
                     reads=c0k + CONST, writes=tkeys)
                k.op("dve", lambda e, tap=tap, c1_=c1_, w1=w1: e.scalar_tensor_tensor(
                    out=tap, in0=c1_, scalar=w1, in1=tap, op0=ALU.mult, op1=ALU.add), reads=c1k + tkeys + CONST, writes=tkeys)
                k.op("dve", lambda e, tap=tap, c2=c2, w2=w2: e.scalar_tensor_tensor(
                    out=tap, in0=c2, scalar=w2, in1=tap, op0=ALU.mult, op1=ALU.add), reads=c2k + tkeys + CONST, writes=tkeys)
                if pend is not None:
                    part_b(*pend[:2])
                    for sl in pend[2]:
                        ws.release(sl)
                pend = (i, wb, (wc, wv, wb))
            part_b(*pend[:2])
            for sl in pend[2]:
                ws.release(sl)

            def rhs(kk, tb):
                lo, hi = tbr(tb)
                return tuple_list(YO(kk, (lo, hi)))
            out_proj(lambda c: [("lin", "od_w_out", 0, c)], rhs, 8, l, n_post, False, nxt)

        def attention(l, n_post, nxt):
            for tb in range(NTB):
                need_xn(tb)
            m32, m32k = M32((0, 8), (0, MEM))
            k.dma("sp", lambda e: e.dma_start(out=m32, in_=memT.rearrange("(c p) m -> p c m", p=128)), "ml", writes=m32k)
            mst = {"pend": [], "mb": None, "gi": 0}

            def mem_sq(c):
                map_, mk = M32(c, (0, MEM))
                i = k.rot("SQD", 3)
                k.op("act", lambda e: e.activation(out=SQD[i][:, 0:MEM], in_=map_, func=AF.Square),
                     reads=mk, writes=[("SQD", i)])
                mst["pend"].append((c, i))

            def mem_mm():
                c, i = mst["pend"].pop(0)
                if mst["mb"] is None:
                    mst["mb"] = k.bank(reserve=True)
                mb = mst["mb"]
                k.mm(P[mb][:, 0:MEM], [(ONESM[:, :], SQD[i][:, 0:MEM], [("SQD", i), ("ONES",)])], mb,
                     start=(c == 0), stop=(c == 7))

            def mem_apply():
                r = rstd_from(mst["mb"], MEM, 1.0)
                for c in range(8):
                    map_, mk = M32(c, (0, MEM))
                    oap, okeys = MN(c, (0, MEM))
                    col = CF[:, CF_MG + l * 8 + c:CF_MG + l * 8 + c + 1]
                    k.op("dve", lambda e, map_=map_, oap=oap, col=col: e.scalar_tensor_tensor(
                        out=oap, in0=map_, scalar=col, in1=R[r][:, 0:MEM], op0=ALU.mult, op1=ALU.mult),
                        reads=mk + [("R", r)] + CONST, writes=okeys)
            for hf, c8 in [(0, c_) for c_ in range(8)]:
                w = ws.acquire(("lin", "xa_wq", l, c8))
                for tb in (2 * hf, 2 * hf + 1):
                    lo, hi = tbr(tb)
                    b = k.bank()
                    k.mm(P[b][:, :], [(w.w(kk),) + tuple(_rk(XN(kk, (lo, hi)), w.key)) for kk in range(8)], b)
                    oap, okeys = Q(c8, (lo, hi))
                    evac_copy(oap, okeys, b, eng="act")
                    gi = mst["gi"]
                    mst["gi"] += 1
                    if 6 <= gi < 14:
                        mem_mm()
                    if 4 <= gi < 12:
                        mem_sq(gi - 4)
                    if gi == 14:
                        mem_apply()
                ws.release(w)
            for hf, c8 in [(1, c_) for c_ in range(4)]:
                w = ws.acquire(("lin", "xa_wq", l, c8))
                for tb in (2 * hf, 2 * hf + 1):
                    lo, hi = tbr(tb)
                    b = k.bank()
                    k.mm(P[b][:, :], [(w.w(kk),) + tuple(_rk(XN(kk, (lo, hi)), w.key)) for kk in range(8)], b)
                    oap, okeys = Q(c8, (lo, hi))
                    evac_copy(oap, okeys, b)
                ws.release(w)
            for c8 in range(8):
                w = ws.acquire(("lin", "xa_wkv", l, c8))
                b = k.bank()
                k.mm(P[b][:, 0:MEM], [(w.w(kk),) + tuple(_rk(MN(kk, (0, MEM)), w.key)) for kk in range(8)], b)
                oap, okeys = KT(c8, (0, MEM))
                evac_copy(oap, okeys, b, n=MEM)
                ws.release(w)
            for cb in range(4):
                wv0 = ws.acquire(("wv", l, cb, 0))
                wv1 = ws.acquire(("wv", l, cb, 1))
                for mt in range(2):
                    b = k.bank()
                    pairs = []
                    for kk in range(8):
                        map_, mk = MN(kk, (mt * 128, (mt + 1) * 128))
                        sl = wv0 if kk < 4 else wv1
                        pairs.append((map_, sl.t[:, (kk % 4) * 256:(kk % 4 + 1) * 256], mk + [sl.key]))
                    k.mm(P[b][:, 0:256], pairs, b)
                    oap, okeys = V(mt, (cb * 256, (cb + 1) * 256))
                    evac_copy(oap, okeys, b, n=256)
                ws.release(wv0)
                ws.release(wv1)
            for hf, c8 in [(1, c_) for c_ in range(4, 8)]:
                w = ws.acquire(("lin", "xa_wq", l, c8))
                for tb in (2 * hf, 2 * hf + 1):
                    lo, hi = tbr(tb)
                    b = k.bank()
                    k.mm(P[b][:, :], [(w.w(kk),) + tuple(_rk(XN(kk, (lo, hi)), w.key)) for kk in range(8)], b)
                    oap, okeys = Q(c8, (lo, hi))
                    evac_copy(oap, okeys, b)
                ws.release(w)
            for sb_ in range(NTB):
                lo, hi = tbr(sb_)
                for h in range(4):
                    et = ET[k.rot("ET", 3)]
                    for mt in range(2):
                        b = k.bank()
                        pairs = []
                        for dc in range(2):
                            kap, kk_ = KT(2 * h + dc, (mt * 128, (mt + 1) * 128))
                            qap, qk = Q(2 * h + dc, (lo, hi))
                            pairs.append((kap, qap, kk_ + qk))
                        k.mm(P[b][:, :], pairs, b)
                        eap, ek = et(mt, (0, TB))
                        k.op("act", lambda e, b=b, eap=eap: e.activation(out=eap, in_=P[b][:, :], func=AF.Exp, scale=1.0 / 16.0),
                             reads=[("P", b)], writes=ek)

                    def stage2(h=h, et=et, lo=lo, hi=hi):
                        b = k.bank()
                        pairs = []
                        for mt in range(2):
                            eap, ek = et(mt, (0, TB))
                            pairs.append((ONES1[:, :], eap, ek + CONST))
                        k.mm(P[b][:, :], pairs, b)
                        rd = RD[k.rot("RD", 2)]
                        rap, rk = rd((0, TB))
                        k.op("act", lambda e: e.activation(out=rap, in_=P[b][:, :], func=AF.Ln), reads=[("P", b)], writes=rk)
                        k.op("act", lambda e: e.activation(out=rap, in_=rap, func=AF.Exp, scale=-1.0), reads=rk, writes=rk)
                        for dc in range(2):
                            b2 = k.bank()
                            pairs = []
                            for mt in range(2):
                                vap, vk = V(mt, (h * 256 + dc * 128, h * 256 + (dc + 1) * 128))
                                eap, ek = et(mt, (0, TB))
                                pairs.append((vap, eap, vk + ek))
                            k.mm(P[b2][:, :], pairs, b2)
                            oap, okeys = O(2 * h + dc, (lo, hi))
                            k.op("dve", lambda e, b2=b2, oap=oap: e.tensor_tensor(out=oap, in0=P[b2][:, :], in1=rap, op=ALU.mult),
                                 reads=[("P", b2)] + rk, writes=okeys)
                    akeys = et(0, (0, TB))[1] + et(1, (0, TB))[1] + RD[0]((0, TB))[1] + RD[1]((0, TB))[1] \
                        + O(2 * h, (lo, hi))[1] + O(2 * h + 1, (lo, hi))[1]
                    k.defer(2, ("att",), stage2, akeys, akeys)

            def rhs(kk, tb):
                lo, hi = tbr(tb)
                return tuple_list(O(kk, (lo, hi)))
            out_proj(lambda c: [("lin", "xa_wo", l, c)], rhs, 8, l, n_post, False, nxt, ho2=lambda c, tl: HO2(c, tl))

        subs = []
        for l in range(L):
            subs += [("ffn", l, 0, 0, 1), ("mix", l, None, 2, 3), ("att", l, None, 4, 5), ("ffn", l, 1, 6, 7)]
        subs = subs[:nsub]
        prenorm_chain(0, 0, list(range(NTB)))
        for si, (kind, l, f, n_pre, n_post) in enumerate(subs):
            nxt = (subs[si + 1][1], subs[si + 1][3]) if si + 1 < len(subs) else None
            if kind == "ffn":
                ffn(l, f, n_post, nxt)
            elif kind == "mix":
                (even_mixer if l % 2 == 0 else odd_mixer)(l, n_post, nxt)
            else:
                attention(l, n_post, nxt)
        k.flush()
        assert k.cnt.get("yo", 0) == 16 * NTB
        k.wait_all("sp", ["yo"])

    def _rk(apk, key):
        return (apk[0], apk[1] + [key])

    def tuple_list(apk):
        return (apk[0], list(apk[1]))

    kd = K(None)
    program(kd)
    sched = list(kd.ws.log)
    k = K(sched)
    program(k)

    semh = {name: nc.alloc_semaphore("s_" + name) for name in k.cnt}

    def replay(eng_name, e):
        for waits, fn, inc in k.q[eng_name]:
            for s_, v in waits:
                e.wait_ge(semh[s_], v)
            if fn is None:
                continue
            ins = fn(e)
            if inc is not None:
                ins.then_inc(semh[inc[0]], inc[1])

    with nc.Block() as block:
        @block.sync
        def _(e):
            replay("sp", e)

        @block.gpsimd
        def _(e):
            replay("pool", e)

        @block.scalar
        def _(e):
            replay("act", e)

        @block.vector
        def _(e):
            replay("dve", e)

        @block.tensor
        def _(e):
            replay("pe", e)
    return nc


_NC_CACHE = {}


def run(inputs, nsub=8, cores=NCORES, trace=False):
    inp = {k_: np.asarray(v, dtype=np.float32) for k_, v in inputs.items()}
    wflat = pack_weights(inp)
    cf, cb = pack_consts(inp)
    alt = (((-1.0) ** np.arange(S)) / np.sqrt(float(S))).astype(np.float32).reshape(1, S)
    if nsub not in _NC_CACHE:
        _NC_CACHE[nsub] = build(nsub)
    nc = _NC_CACHE[nsub]
    in_maps = []
    for b in range(cores):
        in_maps.append({
            "xT": np.ascontiguousarray(inp["x"][b].T),
            "memT": np.ascontiguousarray(inp["mem"][b].T),
            "wflat": wflat, "cf": cf, "cb": cb, "alt": alt,
        })
    res = run_bass_kernel_spmd(nc, in_maps, core_ids=list(range(cores)), trace=trace)
    out = np.stack([np.ascontiguousarray(res.results[b]["yT"].T) for b in range(cores)], axis=0)
    return out.astype(np.float32), res


def kernel(**inputs):
    out, _ = run(inputs)
    return out
```

```python
import numpy as np
import concourse.bass as bass
import concourse.mybir as mybir
from concourse.bass_utils import run_bass_kernel_spmd

F32 = mybir.dt.float32
BF16 = mybir.dt.bfloat16
AF = mybir.ActivationFunctionType
ALU = mybir.AluOpType

D = 1024
S = 2048
DFF = 2816
NJ = DFF // 128
MEM = 256
L = 2
TB = 512
NTB = 4
EPS = 1e-6
NSLOT = 8
SLOT_ELEMS = 1024
NCORES = 8

SB_BASE = 16512
SB_TOP = 229344
OFF_X = 0
OFF_ARENA = OFF_X + 65536
ARENA_BYTES = 110592
OFF_SLOTS = OFF_ARENA + ARENA_BYTES
OFF_SQ = OFF_SLOTS + NSLOT * 2048
OFF_R = OFF_SQ + 7 * 1024
OFF_S = OFF_R + 2 * 2048
OFF_CF = OFF_S + 2 * 2048
NCF = 128 + 16 + 124 + 4 + 4 + 4 + 24
CF_G, CF_MG, CF_DW, CF_DWB, CF_GNG, CF_GNB, CF_OD = 0, 128, 144, 268, 272, 276, 280
OFF_CB = OFF_CF + NCF * 4
NCB = 512 + 256 + 128
CB_FW, CB_CSC, CB_ID = 0, 512, 768
OFF_ONES = OFF_CB + NCB * 2
OFF_END = OFF_ONES + 256 + 256 + 512
assert SB_BASE + OFF_END <= SB_TOP, (SB_BASE + OFF_END, SB_TOP)

A_XN = 0
A_H = 32768
A_HO = 77824


def _pkc(mat):
    nk = mat.shape[0] // 128
    return np.ascontiguousarray(mat.reshape(nk, 128, mat.shape[1]).transpose(1, 0, 2)).reshape(128, -1)


def _dft_tables():
    n = np.arange(S, dtype=np.int64)
    m = (n[:, None] * n[None, :]) % S
    ang = 2.0 * np.pi * m.astype(np.float64) / S
    cs = (np.cos(ang) / np.sqrt(S)).astype(np.float32)
    ns = (-np.sin(ang) / np.sqrt(S)).astype(np.float32)
    return cs, ns


def panel_specs():
    specs = []
    for l in range(L):
        for f in range(2):
            for j in range(NJ):
                for nm, gu in (("ffn_w_gate", "g"), ("ffn_w_up", "u")):
                    specs.append((("gu", l, f, j, gu), 1024,
                                  lambda inp, c, nm=nm, l=l, f=f, j=j: _pkc(inp[nm][l, f][:, j * 128:(j + 1) * 128])))
            for c in range(8):
                for pc, (r0, r1) in enumerate(((0, 1024), (1024, 2048), (2048, 2816))):
                    specs.append((("d", l, f, c, pc), (r1 - r0),
                                  lambda inp, cc, l=l, f=f, c=c, r0=r0, r1=r1:
                                  _pkc(inp["ffn_w_down"][l, f][r0:r1, c * 128:(c + 1) * 128])))
        for nm, ncol in (("xa_wq", 8), ("xa_wkv", 8), ("xa_wo", 8)):
            for c in range(ncol):
                specs.append((("lin", nm, l, c), 1024,
                              lambda inp, cc, nm=nm, l=l, c=c: _pkc(inp[nm][l][:, c * 128:(c + 1) * 128])))
        for cb in range(4):
            for kh in range(2):
                specs.append((("wv", l, cb, kh), 1024,
                              lambda inp, cc, l=l, cb=cb, kh=kh:
                              _pkc(inp["xa_wkv"][l][kh * 512:(kh + 1) * 512, 1024 + cb * 256:1024 + (cb + 1) * 256])))
    for nm, ncol in (("ev_w_in", 12), ("ev_w_out", 8), ("od_w_in", 24), ("od_w_out", 8)):
        for c in range(ncol):
            specs.append((("lin", nm, 0, c), 1024,
                          lambda inp, cc, nm=nm, c=c: _pkc(inp[nm][0][:, c * 128:(c + 1) * 128])))
    for which in range(2):
        for kb in range(4):
            for ttp in range(4):
                specs.append((("dft", which, kb, ttp), 1024,
                              lambda inp, cc, which=which, kb=kb, ttp=ttp:
                              _pkc(cc["dft"][which][ttp * 256:(ttp + 1) * 256, kb * 512:(kb + 1) * 512])))
    return specs


_SPECS = panel_specs()
PANEL = {}
_off = 0
for _k, _n, _g in _SPECS:
    PANEL[_k] = (_off, _n)
    _off += 128 * _n
WFLAT_ELEMS = _off


def pack_weights(inp):
    cache = {"dft": _dft_tables()}
    flat = np.empty(WFLAT_ELEMS, dtype=np.float32)
    for key, n, g in _SPECS:
        off, _ = PANEL[key]
        a = g(inp, cache)
        assert a.shape == (128, n), (key, a.shape)
        flat[off:off + 128 * n] = a.reshape(-1)
    return flat


def pack_consts(inp):
    cf = np.zeros((128, NCF), np.float32)
    ng = inp["norm_g"]
    for l in range(L):
        for n in range(8):
            for c in range(8):
                cf[:, CF_G + (l * 8 + n) * 8 + c] = ng[l, n, c * 128:(c + 1) * 128]
        for c in range(8):
            cf[:, CF_MG + l * 8 + c] = inp["mem_norm_g"][l, c * 128:(c + 1) * 128]
    for i in range(4):
        for j in range(31):
            cf[:, CF_DW + i * 31 + j] = inp["ev_dw_w"][0, j, i * 128:(i + 1) * 128]
        cf[:, CF_DWB + i] = inp["ev_dw_b"][0, i * 128:(i + 1) * 128]
        cf[:, CF_GNG + i] = inp["ev_gn_g"][0, i * 128:(i + 1) * 128]
        cf[:, CF_GNB + i] = inp["ev_gn_b"][0, i * 128:(i + 1) * 128]
    for i in range(8):
        for j in range(3):
            cf[:, CF_OD + i * 3 + j] = inp["od_conv_w"][0, j, i * 128:(i + 1) * 128]
    cb = np.zeros((128, NCB), np.float32)
    for g in range(4):
        cb[:, CB_FW + g * 128:CB_FW + (g + 1) * 128] = inp["ev_fourier_w"][0, g]
    c = np.arange(128)
    ang = 2.0 * np.pi * ((c[:, None] * c[None, :]) % 128) / 128.0
    cb[:, CB_CSC:CB_CSC + 128] = np.cos(ang) / np.sqrt(128.0)
    cb[:, CB_CSC + 128:CB_CSC + 256] = np.sin(ang) / np.sqrt(128.0)
    cb[:, CB_ID:CB_ID + 128] = np.eye(128)
    return cf, cb


class K:
    ENGS = ("pe", "act", "dve", "pool", "sp")

    def __init__(self, sched=None):
        self.q = {e: [] for e in self.ENGS}
        self.cnt = {}
        self.seen = {}
        self.tr = {}
        self.deferred = []
        self._in_def = False
        self._bank = 0
        self.reserved = set()
        self._rot = {}
        self.load = {"act": 0, "dve": 0}
        self.ws = WS(self, sched)

    def _collect(self, eng, reads, writes):
        need = {}
        for k_ in reads:
            e = self.tr.get(k_)
            if e and e[0]:
                s, v = e[0]
                if need.get(s, 0) < v:
                    need[s] = v
        for k_ in writes:
            e = self.tr.get(k_)
            if e:
                if e[0]:
                    s, v = e[0]
                    if need.get(s, 0) < v:
                        need[s] = v
                for s, v in e[1].items():
                    if need.get(s, 0) < v:
                        need[s] = v
        out = []
        for s, v in need.items():
            if eng == "pe" and s == "pe":
                continue
            if self.seen.get((eng, s), 0) >= v:
                continue
            self.seen[(eng, s)] = v
            out.append((s, v))
        return out

    def _commit(self, sem, val, reads, writes):
        for k_ in writes:
            self.tr[k_] = [(sem, val), {}]
        for k_ in reads:
            e = self.tr.get(k_)
            if e is None:
                e = self.tr[k_] = [None, {}]
            if e[1].get(sem, 0) < val:
                e[1][sem] = val

    def op(self, eng, fn, reads=(), writes=()):
        self._check(reads, writes)
        waits = self._collect(eng, reads, writes)
        self.cnt[eng] = self.cnt.get(eng, 0) + 1
        self.q[eng].append((waits, fn, (eng, 1)))
        self._commit(eng, self.cnt[eng], reads, writes)

    def dma(self, eng, fn, sem, reads=(), writes=()):
        self._check(reads, writes)
        waits = self._collect(eng, reads, writes)
        self.cnt[sem] = self.cnt.get(sem, 0) + 16
        self.q[eng].append((waits, fn, (sem, 16)))
        self._commit(sem, self.cnt[sem], reads, writes)

    def wait_all(self, eng, sems):
        self.q[eng].append(([(s, self.cnt[s]) for s in sems if self.cnt.get(s, 0) > 0], None, None))

    def mm(self, out, pairs, bank, start=True, stop=True):
        n = len(pairs)
        allreads = []
        bk = ("P", bank)
        self._check([k_ for p_ in pairs for k_ in p_[2]], [bk])
        for i, (l_, r_, rd) in enumerate(pairs):
            last = i == n - 1
            waits = self._collect("pe", rd, [bk] if (i == 0 and start) else [])
            st = bool(start and i == 0)
            sp = bool(stop and last)
            fn = (lambda e, l_=l_, r_=r_, st=st, sp=sp: e.matmul(out, lhsT=l_, rhs=r_, start=st, stop=sp))
            inc = None
            if last:
                self.cnt["pe"] = self.cnt.get("pe", 0) + 1
                inc = ("pe", 1)
            self.q["pe"].append((waits, fn, inc))
            allreads.extend(rd)
        self._commit("pe", self.cnt["pe"], allreads, [bk])
        self._tick()

    def defer(self, n, tag, fn, rkeys=(), wkeys=()):
        self.deferred.append([n, tag, fn, set(rkeys) | set(wkeys), set(wkeys)])

    def _check(self, reads, writes):
        if self._in_def:
            return
        while self.deferred:
            hit = -1
            for i, d in enumerate(self.deferred):
                rw, w = d[3], d[4]
                if not rw:
                    continue
                if any(k_ in rw for k_ in writes) or any(k_ in w for k_ in reads):
                    hit = i
            if hit < 0:
                return
            for _ in range(hit + 1):
                self._run_head()

    def _run_head(self):
        n, tag, fn, _rw, _w = self.deferred.pop(0)
        prev = self._in_def
        self._in_def = True
        fn()
        self._in_def = prev

    def _tick(self):
        if self._in_def:
            return
        for d in self.deferred:
            d[0] -= 1
        while self.deferred and self.deferred[0][0] <= 0:
            self._run_head()

    def need(self, tag):
        idx = [i for i, d in enumerate(self.deferred) if d[1] == tag]
        if not idx:
            return
        last = idx[-1]
        target = self.deferred[last]
        while any(d is target for d in self.deferred):
            self._run_head()

    def flush(self):
        while self.deferred:
            self._run_head()

    def bank(self, reserve=False):
        for _ in range(8):
            b = self._bank
            self._bank = (self._bank + 1) % 8
            if b not in self.reserved:
                if reserve:
                    self.reserved.add(b)
                return b
        raise RuntimeError("no free PSUM bank")

    def rot(self, name, n):
        i = self._rot.get(name, 0)
        self._rot[name] = (i + 1) % n
        return i

    def evac_eng(self, elems):
        e = "act" if self.load["act"] <= self.load["dve"] else "dve"
        self.load[e] += elems
        return e


class Slot:
    def __init__(self, pos, idx, t):
        self.pos, self.idx, self.t = pos, idx, t
        self.key = ("W", idx)

    def w(self, kk):
        return self.t[:, kk * 128:(kk + 1) * 128]


class WS:
    def __init__(self, k, sched):
        self.k, self.sched = k, sched
        self.log = []
        self.pos = 0
        self.issued = 0
        self.released = 0
        self.slots = None
        self.wflat = None

    def _issue(self):
        i = self.issued
        idx = i % NSLOT
        off, n = PANEL[self.sched[i]]
        src = self.wflat[off:off + 128 * n].rearrange("(p n) -> p n", p=128)
        dst = self.slots[idx][:, 0:n]
        self.k.dma("pool", lambda e, dst=dst, src=src: e.dma_start(out=dst, in_=src), "w%d" % idx,
                   reads=([("X", c, 0) for c in range(4)] if i == 0 else ()), writes=[("W", idx)])
        self.issued += 1

    def acquire(self, pkey):
        self.log.append(pkey)
        pos = self.pos
        self.pos += 1
        if self.sched is None:
            return Slot(pos, pos % NSLOT, self.slots[pos % NSLOT])
        assert self.sched[pos] == pkey, (pos, self.sched[pos], pkey)
        while self.issued < min(len(self.sched), max(NSLOT, pos + 1)) and self.issued < self.released + NSLOT:
            self._issue()
        assert self.issued > pos
        return Slot(pos, pos % NSLOT, self.slots[pos % NSLOT])

    def release(self, slot):
        assert slot.pos == self.released, (slot.pos, self.released)
        self.released += 1
        if self.sched is None:
            return
        while self.issued < len(self.sched) and self.issued < self.released + NSLOT:
            self._issue()


class AT:
    def __init__(self, nc, name, off, shape, dt):
        self.esz = 2 if dt == BF16 else 4
        self.off = off
        self.shape = list(shape)
        nbytes = int(np.prod(shape)) * self.esz
        assert off + nbytes <= ARENA_BYTES, (name, off, nbytes)
        self.h = nc.alloc_sbuf_tensor_at(name, [128] + list(shape), dt, offset=SB_BASE + OFF_ARENA + off)
        self.strides = [int(np.prod(shape[i + 1:])) for i in range(len(shape))]

    def __call__(self, *idx):
        assert len(idx) == len(self.shape)
        sl = [slice(None)]
        for i in idx:
            sl.append(i if isinstance(i, int) else slice(i[0], i[1]))
        ap = self.h[tuple(sl)]
        lead = idx[:-1]
        last = idx[-1]
        lo, hi = (last, last + 1) if isinstance(last, int) else last
        combos = [0]
        for d, i in enumerate(lead):
            rng = [i] if isinstance(i, int) else range(i[0], i[1])
            combos = [c + r * self.strides[d] for c in combos for r in rng]
        keys = set()
        for c in combos:
            b0 = (self.off + (c + lo) * self.esz) // 1024
            b1 = (self.off + (c + hi) * self.esz - 1) // 1024
            for b in range(b0, b1 + 1):
                keys.add(("A", b))
        return ap, list(keys)


def build(nsub=8):
    nc = bass.Bass("TRN2", target_bir_lowering=False)
    xT = nc.dram_tensor("xT", [D, S], F32, kind="ExternalInput").ap()
    memT = nc.dram_tensor("memT", [D, MEM], F32, kind="ExternalInput").ap()
    wflat = nc.dram_tensor("wflat", [WFLAT_ELEMS], F32, kind="ExternalInput").ap()
    cfd = nc.dram_tensor("cf", [128, NCF], F32, kind="ExternalInput").ap()
    cbd = nc.dram_tensor("cb", [128, NCB], F32, kind="ExternalInput").ap()
    altd = nc.dram_tensor("alt", [1, S], F32, kind="ExternalInput").ap()
    yT = nc.dram_tensor("yT", [D, S], F32, kind="ExternalOutput").ap()

    def sb(name, off, shape, dt):
        return nc.alloc_sbuf_tensor_at(name, [128] + list(shape), dt, offset=SB_BASE + off)

    X = sb("X", OFF_X, [8, S], F32)
    SLOTS = [sb("slot%d" % i, OFF_SLOTS + i * 2048, [SLOT_ELEMS], BF16) for i in range(NSLOT)]
    SQD = [sb("sqd%d" % i, OFF_SQ + i * 1024, [TB], BF16) for i in range(3)]
    SQP = sb("sqp", OFF_SQ + 3 * 1024, [4, TB], BF16)
    R = [sb("r%d" % i, OFF_R + i * 2048, [TB], F32) for i in range(2)]
    SS = [sb("s%d" % i, OFF_S + i * 2048, [TB], F32) for i in range(2)]
    CF = sb("cf_sb", OFF_CF, [NCF], F32)
    CB = sb("cb_sb", OFF_CB, [NCB], BF16)
    ONESM = sb("onesm", OFF_ONES, [128], BF16)
    ONES1 = sb("ones1", OFF_ONES + 256, [128], BF16)
    ONESF = sb("onesf", OFF_ONES + 512, [128], F32)
    P = [nc.alloc_psum_tensor("ps%d" % i, [128, TB], F32) for i in range(8)]

    XN = AT(nc, "XN", A_XN, [8, S], BF16)
    H = AT(nc, "H", A_H, [NJ, 1024], BF16)
    HO = AT(nc, "HO", A_HO, [8, 1024], F32)
    UF = AT(nc, "UF", 32768, [4, S], BF16)
    YA = AT(nc, "YA", 32768, [4, S], BF16)
    CC = AT(nc, "CC", 49152, [4, S + 32], BF16)
    FB = AT(nc, "FB", 49152 + 16640, [4, TB], BF16)
    AB = AT(nc, "AB", 0, [9, 4, 256], BF16)
    USY = AT(nc, "USY", A_HO, [4, 1152], BF16)
    UAS = AT(nc, "UAS", A_HO + 9216, [4, 1024], BF16)
    ALT = AT(nc, "ALT", A_HO + 17408, [S], BF16)
    YB = AT(nc, "YB", 0, [4, S], BF16)
    DG = [AT(nc, "DG0", 69888, [31, 128], BF16), AT(nc, "DG1", A_HO + 24576, [31, 128], BF16)]
    GS = {nm: [AT(nc, "gs_%s%d" % (nm, i), A_HO + (j * 3 + i) * 2048, [TB], F32) for i in range(3)]
          for j, nm in enumerate(("c1", "d", "d2", "sd"))}
    GSB = {"c1b": [AT(nc, "gsb_c1b%d" % i, A_HO + (3 * 3 + i) * 2048, [TB], BF16) for i in range(3)],
           "d2b": [AT(nc, "gsb_d2b%d" % i, A_HO + (2 * 3 + i) * 2048, [TB], BF16) for i in range(3)]}
    YO = AT(nc, "YO", 32768, [8, S], BF16)
    CV = [AT(nc, "CV%d" % i, 65536 + i * 8448, [S + 64], F32) for i in range(2)]
    TT = [AT(nc, "TT%d" % i, 65536 + 2 * 8448 + i * 8192, [S], F32) for i in range(2)]
    O = AT(nc, "O", 0, [8, S], BF16)
    HO2 = AT(nc, "HO2", 32768, [8, 1024], F32)
    Q = AT(nc, "Q", 32768, [8, S], BF16)
    M32 = AT(nc, "M32", 65536, [8, MEM], F32)
    MN = AT(nc, "MN", 73728, [8, MEM], BF16)
    KT = AT(nc, "KT", 65536, [8, MEM], BF16)
    V = AT(nc, "V", 69632, [2, D], BF16)
    ET = [AT(nc, "ET%d" % i, 86016 + i * 2048, [2, TB], BF16) for i in range(3)]
    RD = [AT(nc, "RD%d" % i, 92160 + i * 2048, [TB], F32) for i in range(2)]

    def tbr(tb):
        return (tb * TB, (tb + 1) * TB)

    def G(l, n, c):
        i = CF_G + (l * 8 + n) * 8 + c
        return CF[:, i:i + 1]

    def program(k):
        ws = k.ws
        ws.slots = SLOTS
        ws.wflat = wflat
        xk = lambda c, tb: ("X", c, tb)

        k.dma("sp", lambda e: e.dma_start(out=CF[:, :], in_=cfd), "c0", writes=[("CF",)])
        k.dma("pool", lambda e: e.dma_start(out=CB[:, :], in_=cbd), "c1", writes=[("CB",)])
        xv = xT.rearrange("(c p) t -> p c t", p=128)
        for tb in range(NTB):
            lo, hi = tbr(tb)
            for ch in range(2):
                k.dma("sp", lambda e, lo=lo, hi=hi, ch=ch: e.dma_start(out=X[:, 4 * ch:4 * ch + 4, lo:hi],
                                                                       in_=xv[:, 4 * ch:4 * ch + 4, lo:hi]), "xl%d_%d" % (tb, ch),
                      writes=[xk(c, tb) for c in range(4 * ch, 4 * ch + 4)])
        k.op("dve", lambda e: e.memset(ONESM[:, :], 1.0 / 1024.0), writes=[("ONES",)])
        k.op("dve", lambda e: e.memset(ONES1[:, :], 1.0), writes=[("ONES",)])
        k.op("dve", lambda e: e.memset(ONESF[:, :], 1.0 / 128.0), writes=[("ONES",)])
        CONST = [("CF",), ("CB",), ("ONES",)]

        def rstd_from(bank, n, sc):
            r = k.rot("R", 2)
            k.op("act", lambda e: e.activation(out=R[r][:, 0:n], in_=P[bank][:, 0:n], func=AF.Ln, bias=sc * EPS, scale=sc),
                 reads=[("P", bank)], writes=[("R", r)])
            k.op("act", lambda e: e.activation(out=R[r][:, 0:n], in_=R[r][:, 0:n], func=AF.Exp, scale=-0.5),
                 reads=[("R", r)], writes=[("R", r)])
            k.reserved.discard(bank)
            return r

        def chain_keys(tbs, with_ho):
            xks = [xk(c, tb) for c in range(8) for tb in tbs]
            xnk = []
            for tb in tbs:
                xnk += XN((0, 8), tbr(tb))[1]
            hok = HO((0, 8), (0, 1024))[1] if with_ho else []
            return hok + xks, hok + xks + xnk

        def run_seq(steps):
            while any(d[1] == ("seq",) for d in k.deferred):
                k.need(("seq",))
            n_ = len(steps)
            suf_r = [set() for _ in range(n_ + 1)]
            suf_w = [set() for _ in range(n_ + 1)]
            for i in range(n_ - 1, -1, -1):
                suf_r[i] = suf_r[i + 1] | set(steps[i][1].rk)
                suf_w[i] = suf_w[i + 1] | set(steps[i][1].wk)

            def go(i):
                if i >= n_:
                    return
                d, fn = steps[i]

                def wrapped():
                    fn()
                    go(i + 1)
                if d <= 0:
                    wrapped()
                else:
                    k.defer(d, ("seq",), wrapped, suf_r[i], suf_w[i])
            go(0)

        def keyed(fn, rk=(), wk=()):
            fn.rk = list(rk)
            fn.wk = list(wk)
            return fn

        def both(*fns):
            def fn():
                for f_ in fns:
                    f_()
            return keyed(fn, [k_ for f_ in fns for k_ in f_.rk], [k_ for f_ in fns for k_ in f_.wk])

        def prenorm_steps(l, n, tb):
            lo, hi = tbr(tb)
            st = {}

            def sa():
                k.op("act", lambda e: e.activation(out=SQP[:, :, :], in_=X[:, 0:4, lo:hi], func=AF.Square),
                     reads=[xk(c, tb) for c in range(4)], writes=[("SQP",)])

            def sb_():
                st["b"] = bnk = k.bank(reserve=True)
                for c in range(4):
                    k.mm(P[bnk][:, :], [(ONESM[:, :], SQP[:, c, :], [("SQP",), ("ONES",)])], bnk, start=(c == 0), stop=False)
                k.op("act", lambda e: e.activation(out=SQP[:, :, :], in_=X[:, 4:8, lo:hi], func=AF.Square),
                     reads=[xk(c, tb) for c in range(4, 8)], writes=[("SQP",)])

            def xn_apply(c):
                r = st["r"]
                oap, okeys = XN(c, (lo, hi))
                k.op("dve", lambda e, c=c, oap=oap: e.scalar_tensor_tensor(
                    out=oap, in0=X[:, c, lo:hi], scalar=G(l, n, c), in1=R[r][:, :], op0=ALU.mult, op1=ALU.mult),
                    reads=[xk(c, tb), ("R", r)] + CONST, writes=okeys)

            def sc1():
                bnk = st["b"]
                for c in range(4):
                    k.mm(P[bnk][:, :], [(ONESM[:, :], SQP[:, c, :], [("SQP",), ("ONES",)])], bnk, start=False, stop=(c == 3))
                st["r"] = rstd_from(bnk, TB, 1.0)
                for c in range(4):
                    xn_apply(c)

            def sc2():
                for c in range(4, 8):
                    xn_apply(c)
            keyed(sa, [xk(c, tb) for c in range(4)])
            keyed(sb_, [xk(c, tb) for c in range(4, 8)])
            keyed(sc1, [xk(c, tb) for c in range(4)], XN((0, 4), (lo, hi))[1])
            keyed(sc2, [xk(c, tb) for c in range(4, 8)], XN((4, 8), (lo, hi))[1])
            return sa, sb_, sc1, sc2

        def prenorm_chain(l, n, tbs):
            steps = []
            for tb in tbs:
                sa, sb_, sc1, sc2 = prenorm_steps(l, n, tb)
                steps += [(0 if not steps else 1, sa), (3, sb_), (3, sc1), (1, sc2)]
            run_seq(steps)

        HO_PENDING = []

        class HalfOut:
            def __init__(self, l, n_post, half, tbs, hoap=None):
                self.l, self.n, self.half, self.tbs = l, n_post, half, tbs
                self.hoap = hoap or (lambda c, tl: HO(c, tl))
                self.pst = {}
                self.pend = []

            def evac(self, c, tb, bank):
                for o_ in list(HO_PENDING):
                    if o_ is not self:
                        o_.stat_mm(0)
                if self not in HO_PENDING:
                    HO_PENDING.append(self)
                tl = ((tb % 2) * TB, (tb % 2 + 1) * TB)
                oap, okeys = self.hoap(c, tl)
                gcol = G(self.l, self.n, c)
                k.op("act", lambda e: e.activation(out=oap, in_=P[bank][:, :], func=AF.Copy, scale=gcol),
                     reads=[("P", bank)] + CONST, writes=okeys)
                i = k.rot("SQD", 3)
                k.op("act", lambda e: e.activation(out=SQD[i][:, :], in_=P[bank][:, :], func=AF.Square),
                     reads=[("P", bank)], writes=[("SQD", i)])
                self.pend.append((tb, c, i))
                self.stat_mm(2)

            def stat_mm(self, keep):
                if keep == 0 and self in HO_PENDING:
                    HO_PENDING.remove(self)
                while len(self.pend) > keep:
                    tb, c, i = self.pend.pop(0)
                    if tb not in self.pst:
                        self.pst[tb] = k.bank(reserve=True)
                    bnk = self.pst[tb]
                    k.mm(P[bnk][:, :], [(ONESM[:, :], SQD[i][:, :], [("SQD", i), ("ONES",)])], bnk,
                         start=(c == 0), stop=(c == 7))

            def finish(self, nxt, hf, cpb=1):
                fin_stats = keyed(lambda: self.stat_mm(0), [("SQD", i_) for i_ in range(3)], [])
                ta, tb_ = self.tbs[0], self.tbs[-1]
                sc = 4.0 if self.half else 1.0
                rr = {}
                if hf == 1:
                    cpb = 1

                def add_eng_for(tb):
                    if nxt is None:
                        return "dve"
                    if hf == 0:
                        return "pool"
                    return "pool" if tb == ta else "dve"

                def A(tb, c0):
                    def fn():
                        lo, hi = tbr(tb)
                        tl = ((tb % 2) * TB, (tb % 2 + 1) * TB)
                        if c0 == 0:
                            rr[tb] = rstd_from(self.pst[tb], TB, sc)
                        r = rr[tb]
                        for c in range(c0, c0 + cpb):
                            hap, hkeys = self.hoap(c, tl)
                            k.op("dve", lambda e, hap=hap: e.tensor_tensor(out=hap, in0=hap, in1=R[r][:, :], op=ALU.mult),
                                 reads=hkeys + [("R", r)], writes=hkeys)
                            k.op(add_eng_for(tb), lambda e, hap=hap, c=c: e.tensor_tensor(out=X[:, c, lo:hi], in0=X[:, c, lo:hi], in1=hap,
                                                                                         op=ALU.add),
                                 reads=hkeys + [xk(c, tb)], writes=[xk(c, tb)])
                    tl_ = ((tb % 2) * TB, (tb % 2 + 1) * TB)
                    ks = [xk(c, tb) for c in range(c0, c0 + cpb)]
                    for c in range(c0, c0 + cpb):
                        ks += self.hoap(c, tl_)[1]
                    return keyed(fn, ks, ks)
                cs = list(range(0, 8, cpb))
                steps = [(2, fin_stats)] + [(1, A(ta, c0)) for c0 in cs]
                if len(self.tbs) == 1:
                    assert nxt is None
                    steps += [(0, keyed(lambda: store_out(ta), [xk(c, ta) for c in range(8)]))]
                elif nxt is None:
                    steps += [(0, keyed(lambda: store_out(ta), [xk(c, ta) for c in range(8)]))]
                    steps += [(1, A(tb_, c0)) for c0 in cs]
                    steps += [(0, keyed(lambda: store_out(tb_), [xk(c, tb_) for c in range(8)]))]
                elif cpb == 1:
                    sa0, sb0, sc10, sc20 = prenorm_steps(nxt[0], nxt[1], ta)
                    sa1, sb1, sc11, sc21 = prenorm_steps(nxt[0], nxt[1], tb_)
                    steps += [(1, both(A(tb_, 0), sa0)), (1, A(tb_, 1)), (1, A(tb_, 2)), (1, both(A(tb_, 3), sb0)),
                              (1, A(tb_, 4)), (1, A(tb_, 5)), (1, both(A(tb_, 6), sc10)), (1, both(A(tb_, 7), sc20)),
                              (2, sa1), (3, sb1), (3, sc11), (1, sc21)]
                else:
                    assert cpb == 2
                    def A2(c0):
                        def fn():
                            for tb in (ta, tb_):
                                if c0 == 0:
                                    rr[tb] = rstd_from(self.pst[tb], TB, sc)
                            for c in (c0, c0 + 1):
                                for tb in (ta, tb_):
                                    lo, hi = tbr(tb)
                                    tl = ((tb % 2) * TB, (tb % 2 + 1) * TB)
                                    r = rr[tb]
                                    hap, hkeys = self.hoap(c, tl)
                                    k.op("dve", lambda e, hap=hap, r=r: e.tensor_tensor(out=hap, in0=hap, in1=R[r][:, :], op=ALU.mult),
                                         reads=hkeys + [("R", r)], writes=hkeys)
                                    k.op("pool" if (tb == ta and hf == 0) else "dve",
                                         lambda e, hap=hap, c=c, lo=lo, hi=hi: e.tensor_tensor(
                                             out=X[:, c, lo:hi], in0=X[:, c, lo:hi], in1=hap, op=ALU.add),
                                         reads=hkeys + [xk(c, tb)], writes=[xk(c, tb)])
                        ks = []
                        for c in (c0, c0 + 1):
                            for tb in (ta, tb_):
                                ks += [xk(c, tb)] + self.hoap(c, ((tb % 2) * TB, (tb % 2 + 1) * TB))[1]
                        return keyed(fn, ks, ks)
                    sa0, sb0, sc10, sc20 = prenorm_steps(nxt[0], nxt[1], ta)
                    sa1, sb1, sc11, sc21 = prenorm_steps(nxt[0], nxt[1], tb_)
                    steps = [(2, fin_stats), (1, A2(0)), (1, A2(2)), (1, A2(4)), (1, A2(6)),
                             (9, sa0), (11, sb0), (3, sc10), (1, both(sc20, sa1)), (3, sb1), (3, sc11), (1, sc21)]
                run_seq(steps)

        def store_out(tb):
            lo, hi = tbr(tb)
            yv = yT.rearrange("(c p) t -> p c t", p=128)
            k.dma("sp", lambda e: e.dma_start(out=yv[:, :, lo:hi], in_=X[:, :, lo:hi]), "yo",
                  reads=[("X", c, tb) for c in range(8)])

        def evac_copy(oap, okeys, bank, n=TB, eng=None):
            eng = eng or k.evac_eng(n)
            if eng == "act":
                k.op("act", lambda e: e.activation(out=oap, in_=P[bank][:, 0:n], func=AF.Copy),
                     reads=[("P", bank)], writes=okeys)
            else:
                k.op("dve", lambda e: e.tensor_copy(out=oap, in_=P[bank][:, 0:n]), reads=[("P", bank)], writes=okeys)

        def out_proj(pkeys_for_c, rhs, nk, l, n_post, half, nxt, ho2=None):
            for hf in range(2):
                tbs = (2 * hf, 2 * hf + 1)
                held = []
                ho = HalfOut(l, n_post, half, tbs, hoap=(ho2 if hf == 1 else None))

                def emit_held():
                    c_, tb_, b_ = held.pop(0)
                    ho.evac(c_, tb_, b_)
                    k.reserved.discard(b_)
                for c in range(8):
                    slots = [ws.acquire(pk) for pk in pkeys_for_c(c)]
                    for tb in tbs:
                        b = k.bank(reserve=True)
                        pairs = []
                        for kk in range(nk):
                            rap, rkeys = rhs(kk, tb)
                            sl = slots[kk // 8]
                            pairs.append((sl.w(kk % 8), rap, rkeys + [sl.key]))
                        k.mm(P[b][:, :], pairs, b)
                        held.append((c, tb, b))
                        if len(held) > 3:
                            emit_held()
                    for sl in slots:
                        ws.release(sl)
                while held:
                    emit_held()
                ho.finish(nxt, hf, cpb=2)

        def need_xn(tb):
            pass

        def ffn(l, f, n_post, nxt):
            for hf in range(2):
                tbs = (2 * hf, 2 * hf + 1)
                for tb in tbs:
                    need_xn(tb)
                for j in range(NJ):
                    wg = ws.acquire(("gu", l, f, j, "g"))
                    wu = ws.acquire(("gu", l, f, j, "u"))
                    for tb in tbs:
                        lo, hi = tbr(tb)
                        tl = ((tb % 2) * TB, (tb % 2 + 1) * TB)
                        bg = k.bank()
                        pairs = []
                        for kk in range(8):
                            rap, rkeys = XN(kk, (lo, hi))
                            pairs.append((wg.w(kk), rap, rkeys + [wg.key]))
                        k.mm(P[bg][:, :], pairs, bg)
                        s = k.rot("S", 2)
                        k.op("act", lambda e, bg=bg, s=s: e.activation(out=SS[s][:, :], in_=P[bg][:, :], func=AF.Silu),
                             reads=[("P", bg)], writes=[("S", s)])
                        bu = k.bank()
                        pairs = []
                        for kk in range(8):
                            rap, rkeys = XN(kk, (lo, hi))
                            pairs.append((wu.w(kk), rap, rkeys + [wu.key]))
                        k.mm(P[bu][:, :], pairs, bu)
                        hap, hkeys = H(j, tl)
                        k.op("dve", lambda e, bu=bu, s=s, hap=hap: e.tensor_tensor(out=hap, in0=P[bu][:, :], in1=SS[s][:, :],
                                                                                    op=ALU.mult),
                             reads=[("P", bu), ("S", s)], writes=hkeys)
                    ws.release(wg)
                    ws.release(wu)
                if nxt is None and hf == 1:
                    for tb in tbs:
                        tl = ((tb % 2) * TB, (tb % 2 + 1) * TB)
                        ho = HalfOut(l, n_post, True, (tb,))
                        for c in range(8):
                            slots = [ws.acquire(("d", l, f, c, pc)) for pc in range(3)]
                            b = k.bank()
                            pairs = []
                            for j in range(NJ):
                                rap, rkeys = H(j, tl)
                                sl = slots[j // 8]
                                pairs.append((sl.w(j % 8), rap, rkeys + [sl.key]))
                            k.mm(P[b][:, :], pairs, b)
                            ho.evac(c, tb, b)
                            for sl in slots:
                                ws.release(sl)
                        ho.finish(nxt, hf)
                    continue
                ho = HalfOut(l, n_post, True, tbs)
                for c in range(8):
                    slots = [ws.acquire(("d", l, f, c, pc)) for pc in range(3)]
                    for tb in tbs:
                        tl = ((tb % 2) * TB, (tb % 2 + 1) * TB)
                        b = k.bank()
                        pairs = []
                        for j in range(NJ):
                            rap, rkeys = H(j, tl)
                            sl = slots[j // 8]
                            pairs.append((sl.w(j % 8), rap, rkeys + [sl.key]))
                        k.mm(P[b][:, :], pairs, b)
                        ho.evac(c, tb, b)
                    for sl in slots:
                        ws.release(sl)
                ho.finish(nxt, hf)

        def even_mixer(l, n_post, nxt):
            for tb in range(NTB):
                need_xn(tb)
            idap = CB[:, CB_ID:CB_ID + 128]
            def build_dg(i):
                dg = DG[i % 2]
                for j in range(31):
                    dap, dkeys = dg(j, (0, 128))
                    col = CF[:, CF_DW + i * 31 + j:CF_DW + i * 31 + j + 1]
                    k.op("dve", lambda e, dap=dap, col=col: e.tensor_scalar(out=dap, in0=idap, scalar1=col, scalar2=None,
                                                                             op0=ALU.mult),
                         reads=CONST, writes=dkeys)
            for i in range(4):
                for (a, b_) in ((0, 15), (15 + S, S + 32)):
                    ap, keys = CC(i, (a, b_))
                    k.op("pool", lambda e, ap=ap: e.memset(ap, 0.0), writes=keys)
            for hf, i in [(h_, i_) for h_ in range(2) for i_ in range(4)]:
                wgt = ws.acquire(("lin", "ev_w_in", 0, 8 + i))
                wvl = ws.acquire(("lin", "ev_w_in", 0, 4 + i))
                for tb in (2 * hf, 2 * hf + 1):
                    lo, hi = tbr(tb)
                    bg = k.bank()
                    k.mm(P[bg][:, :], [(wgt.w(kk),) + tuple(_rk(XN(kk, (lo, hi)), wgt.key)) for kk in range(8)], bg)
                    s = k.rot("S", 2)
                    k.op("act", lambda e, bg=bg, s=s: e.activation(out=SS[s][:, :], in_=P[bg][:, :], func=AF.Sigmoid),
                         reads=[("P", bg)], writes=[("S", s)])
                    bv = k.bank()
                    k.mm(P[bv][:, :], [(wvl.w(kk),) + tuple(_rk(XN(kk, (lo, hi)), wvl.key)) for kk in range(8)], bv)
                    cap, ckeys = CC(i, (15 + lo, 15 + hi))
                    k.op("dve", lambda e, bv=bv, s=s, cap=cap: e.tensor_tensor(out=cap, in0=P[bv][:, :], in1=SS[s][:, :],
                                                                                op=ALU.mult),
                         reads=[("P", bv), ("S", s)], writes=ckeys)
                ws.release(wgt)
                ws.release(wvl)
            for hf, g in [(h_, g_) for h_ in range(2) for g_ in range(4)]:
                wf = ws.acquire(("lin", "ev_w_in", 0, g))
                for tb in (2 * hf, 2 * hf + 1):
                    lo, hi = tbr(tb)
                    b = k.bank()
                    k.mm(P[b][:, :], [(wf.w(kk),) + tuple(_rk(XN(kk, (lo, hi)), wf.key)) for kk in range(8)], b)
                    oap, okeys = UF(g, (lo, hi))
                    evac_copy(oap, okeys, b, eng=("act" if hf == 0 else None))
                ws.release(wf)
            k.dma("pool", lambda e: e.dma_start(out=ALT.h[0:1, :], in_=altd), "c1", writes=ALT((0, S))[1])
            for g in range(4):
                usy, usyk = USY(g, (1, 1024))
                uas, uask = UAS(g, (1, 1024))
                ufw, ufwk = UF(g, (1, 1024))
                ufr = UF.h[:, g, 2047:1024:-1]
                ufrk = UF(g, (1025, S))[1]
                k.op("dve", lambda e, usy=usy, ufw=ufw, ufr=ufr: e.tensor_tensor(out=usy, in0=ufw, in1=ufr, op=ALU.add),
                     reads=ufwk + ufrk, writes=usyk)
                k.op("dve", lambda e, uas=uas, ufw=ufw, ufr=ufr: e.tensor_tensor(out=uas, in0=ufw, in1=ufr, op=ALU.subtract),
                     reads=ufwk + ufrk, writes=uask)
                a0, a0k = USY(g, (0, 1))
                k.op("dve", lambda e, a0=a0, g=g: e.tensor_copy(out=a0, in_=UF.h[:, g, 0:1]), reads=UF(g, (0, 1))[1], writes=a0k)
                a1, a1k = USY(g, (1024, 1152))
                k.op("dve", lambda e, a1=a1: e.memset(a1, 0.0), writes=a1k)
                a2, a2k = USY(g, (1024, 1025))
                k.op("dve", lambda e, a2=a2, g=g: e.tensor_copy(out=a2, in_=UF.h[:, g, 1024:1025]), reads=UF(g, (1024, 1025))[1],
                     writes=a2k)
                z0, z0k = UAS(g, (0, 1))
                k.op("dve", lambda e, z0=z0: e.memset(z0, 0.0), writes=z0k)
            ccos = CB[:, CB_CSC:CB_CSC + 128]
            csin = CB[:, CB_CSC + 128:CB_CSC + 256]
            for tt in range(9):
                for g in range(4):
                    b = k.bank()
                    uap, ukeys = USY(g, (tt * 128, (tt + 1) * 128))
                    k.mm(P[b][:, 0:128], [(uap, ccos, ukeys + CONST)], b)
                    oap, okeys = AB(tt, g, (0, 128))
                    evac_copy(oap, okeys, b, n=128)
                    if tt < 8:
                        b = k.bank()
                        uap, ukeys = UAS(g, (tt * 128, (tt + 1) * 128))
                        k.mm(P[b][:, 0:128], [(uap, csin, ukeys + CONST)], b)
                        oap, okeys = AB(tt, g, (128, 256))
                        evac_copy(oap, okeys, b, n=128)
            build_dg(0)
            build_dg(1)
            for kb in range(4):
                acc = [k.bank(reserve=True) for g in range(4)]
                for ttp in range(4):
                    for which in range(2):
                        w = ws.acquire(("dft", which, kb, ttp))
                        for g in range(4):
                            pairs = []
                            for ti in range(2):
                                tt = 2 * ttp + ti
                                lap, lkeys = AB(tt, g, (which * 128, (which + 1) * 128))
                                pairs.append((lap, w.t[:, ti * TB:(ti + 1) * TB], lkeys + [w.key]))
                            k.mm(P[acc[g]][:, :], pairs, acc[g], start=(ttp == 0 and which == 0), stop=False)
                        ws.release(w)
                for g in range(4):
                    lkeys = AB(8, g, (0, 128))[1]
                    k.mm(P[acc[g]][:, :], [(AB.h[0:1, 8, g, 0:128], ALT.h[0:1, kb * TB:(kb + 1) * TB],
                                            lkeys + ALT((0, S))[1])], acc[g], start=False, stop=True)

                def step3(kb=kb, acc=acc):
                    lo, hi = tbr(kb)
                    for g in range(4):
                        fap, fkeys = FB(g, (0, TB))
                        evac_copy(fap, fkeys, acc[g])
                        k.reserved.discard(acc[g])
                    for g in range(4):
                        fap, fkeys = FB(g, (0, TB))
                        b = k.bank()
                        k.mm(P[b][:, :], [(CB[:, CB_FW + g * 128:CB_FW + (g + 1) * 128], fap, fkeys + CONST)], b)
                        oap, okeys = YA(g, (lo, hi))
                        evac_copy(oap, okeys, b)
                k.defer(6, ("f3", kb), step3)
            k.flush()
            for i in range(4):
                dg = DG[i % 2]
                for tb in range(NTB):
                    lo, hi = tbr(tb)
                    b = k.bank()
                    pairs = []
                    for j in range(31):
                        dap, dkeys = dg(j, (0, 128))
                        cap, ckeys = CC(i, (lo + j, hi + j))
                        pairs.append((dap, cap, dkeys + ckeys))
                    k.mm(P[b][:, :], pairs, b)
                    gi = k.rot("GS", 3)
                    c1, c1k = GS["c1"][gi]((0, TB))
                    dv, dvk = GS["d"][gi]((0, TB))
                    d2, d2k = GS["d2"][gi]((0, TB))
                    sd, sdk = GS["sd"][gi]((0, TB))
                    c1b, c1bk = GSB["c1b"][gi]((0, TB))
                    d2b, d2bk = GSB["d2b"][gi]((0, TB))
                    k.op("act", lambda e, b=b, c1=c1, i=i: e.activation(out=c1, in_=P[b][:, :], func=AF.Identity,
                                                                        bias=CF[:, CF_DWB + i:CF_DWB + i + 1]),
                         reads=[("P", b)] + CONST, writes=c1k)
                    k.op("act", lambda e, b=b, c1b=c1b, i=i: e.activation(out=c1b, in_=P[b][:, :], func=AF.Identity,
                                                                          bias=CF[:, CF_DWB + i:CF_DWB + i + 1]),
                         reads=[("P", b)] + CONST, writes=c1bk)

                    gkeys = c1k + dvk + d2k + sdk + YB(i, (lo, hi))[1]

                    def stage_b(i=i, lo=lo, hi=hi, c1=c1, c1k=c1k, dv=dv, dvk=dvk, d2=d2, d2k=d2k, sd=sd, sdk=sdk, gkeys=gkeys,
                                c1b=c1b, c1bk=c1bk, d2b=d2b, d2bk=d2bk):
                        b2 = k.bank()
                        k.mm(P[b2][:, :], [(ONES1[:, :], c1b, c1bk + CONST)], b2)
                        k.op("dve", lambda e: e.scalar_tensor_tensor(out=dv, in0=P[b2][:, :], scalar=-1.0 / 128.0, in1=c1,
                                                                     op0=ALU.mult, op1=ALU.add),
                             reads=c1k + [("P", b2)], writes=dvk)
                        k.op("act", lambda e: e.activation(out=d2b, in_=dv, func=AF.Square), reads=dvk, writes=d2bk)

                        def stage_c():
                            b3 = k.bank()
                            k.mm(P[b3][:, :], [(ONES1[:, :], d2b, d2bk + CONST)], b3)
                            k.op("act", lambda e: e.activation(out=sd, in_=P[b3][:, :], func=AF.Ln, bias=EPS, scale=1.0 / 128.0),
                                 reads=[("P", b3)], writes=sdk)
                            k.op("act", lambda e: e.activation(out=sd, in_=sd, func=AF.Exp, scale=-0.5), reads=sdk, writes=sdk)
                            k.op("dve", lambda e: e.scalar_tensor_tensor(
                                out=d2, in0=dv, scalar=CF[:, CF_GNG + i:CF_GNG + i + 1], in1=sd, op0=ALU.mult, op1=ALU.mult),
                                reads=dvk + sdk + CONST, writes=d2k)
                            yap, ykeys = YB(i, (lo, hi))
                            k.op("act", lambda e: e.activation(out=yap, in_=d2, func=AF.Silu,
                                                               bias=CF[:, CF_GNB + i:CF_GNB + i + 1]),
                                 reads=d2k + CONST, writes=ykeys)
                        k.defer(1, ("gln",), stage_c, gkeys, gkeys)
                    k.defer(1, ("gln",), stage_b, gkeys, gkeys)
                if i + 2 < 4:
                    build_dg(i + 2)

            def rhs(kk, tb):
                lo, hi = tbr(tb)
                return tuple_list(YA(kk, (lo, hi)) if kk < 4 else YB(kk - 4, (lo, hi)))
            out_proj(lambda c: [("lin", "ev_w_out", 0, c)], rhs, 8, l, n_post, False, nxt)

        def odd_mixer(l, n_post, nxt):
            for tb in range(NTB):
                need_xn(tb)
            for cvb in range(2):
                for (a, b_) in ((0, 1), (1 + S, S + 64)):
                    ap, keys = CV[cvb]((a, b_))
                    k.op("pool", lambda e, ap=ap: e.memset(ap, 0.0), writes=keys)

            def part_b(i, wb):
                tap, tkeys = TT[i % 2]((0, S))
                for tb in range(NTB):
                    lo, hi = tbr(tb)
                    bb = k.bank()
                    k.mm(P[bb][:, :], [(wb.w(kk),) + tuple(_rk(XN(kk, (lo, hi)), wb.key)) for kk in range(8)], bb)
                    tslice, tk = TT[i % 2]((lo, hi))
                    yap, ykeys = YO(i, (lo, hi))
                    k.op("dve", lambda e, bb=bb, tslice=tslice, yap=yap: e.tensor_tensor(out=yap, in0=P[bb][:, :], in1=tslice,
                                                                                          op=ALU.mult),
                         reads=[("P", bb)] + tk, writes=ykeys)

            pend = None
            for i in range(8):
                wc = ws.acquire(("lin", "od_w_in", 0, 8 + i))
                wv = ws.acquire(("lin", "od_w_in", 0, 16 + i))
                wb = ws.acquire(("lin", "od_w_in", 0, i))
                cv = CV[i % 2]
                for tb in range(NTB):
                    lo, hi = tbr(tb)
                    bc = k.bank()
                    k.mm(P[bc][:, :], [(wc.w(kk),) + tuple(_rk(XN(kk, (lo, hi)), wc.key)) for kk in range(8)], bc)
                    s = k.rot("S", 2)
                    k.op("act", lambda e, bc=bc, s=s: e.activation(out=SS[s][:, :], in_=P[bc][:, :], func=AF.Copy),
                         reads=[("P", bc)], writes=[("S", s)])
                    bv = k.bank()
                    k.mm(P[bv][:, :], [(wv.w(kk),) + tuple(_rk(XN(kk, (lo, hi)), wv.key)) for kk in range(8)], bv)
                    cap, ckeys = cv((1 + lo, 1 + hi))
                    k.op("dve", lambda e, bv=bv, s=s, cap=cap: e.tensor_tensor(out=cap, in0=P[bv][:, :], in1=SS[s][:, :],
                                                                                op=ALU.mult),
                         reads=[("P", bv), ("S", s)], writes=ckeys)
                tap, tkeys = TT[i % 2]((0, S))
                w0 = CF[:, CF_OD + i * 3 + 0:CF_OD + i * 3 + 1]
                w1 = CF[:, CF_OD + i * 3 + 1:CF_OD + i * 3 + 2]
                w2 = CF[:, CF_OD + i * 3 + 2:CF_OD + i * 3 + 3]
                c0, c0k = cv((0, S))
                c1_, c1k = cv((1, S + 1))
                c2, c2k = cv((2, S + 2))
                k.op("dve", lambda e, tap=tap, c0=c0, w0=w0: e.tensor_scalar(out=tap, in0=c0, scalar1=w0, scalar2=None,
                                                                             op0=ALU.mult),
                     reads=c0k + CONST, writes=tkeys)
                k.op("dve", lambda e, tap=tap, c1_=c1_, w1=w1: e.scalar_tensor_tensor(
                    out=tap, in0=c1_, scalar=w1, in1=tap, op0=ALU.mult, op1=ALU.add), reads=c1k + tkeys + CONST, writes=tkeys)
                k.op("dve", lambda e, tap=tap, c2=c2, w2=w2: e.scalar_tensor_tensor(
                    out=tap, in0=c2, scalar=w2, in1=tap, op0=ALU.mult, op1=ALU.add), reads=c2k + tkeys + CONST, writes=tkeys)
                if pend is not None:
                    part_b(*pend[:2])
                    for sl in pend[2]:
                        ws.release(sl)
                pend = (i, wb, (wc, wv, wb))
            part_b(*pend[:2])
            for sl in pend[2]:
                ws.release(sl)

            def rhs(kk, tb):
                lo, hi = tbr(tb)
                return tuple_list(YO(kk, (lo, hi)))
            out_proj(lambda c: [("lin", "od_w_out", 0, c)], rhs, 8, l, n_post, False, nxt)

        def attention(l, n_post, nxt):
            for tb in range(NTB):
                need_xn(tb)
            m32, m32k = M32((0, 8), (0, MEM))
            k.dma("sp", lambda e: e.dma_start(out=m32, in_=memT.rearrange("(c p) m -> p c m", p=128)), "ml", writes=m32k)
            mst = {"pend": [], "mb": None, "gi": 0}

            def mem_sq(c):
                map_, mk = M32(c, (0, MEM))
                i = k.rot("SQD", 3)
                k.op("act", lambda e: e.activation(out=SQD[i][:, 0:MEM], in_=map_, func=AF.Square),
                     reads=mk, writes=[("SQD", i)])
                mst["pend"].append((c, i))

            def mem_mm():
                c, i = mst["pend"].pop(0)
                if mst["mb"] is None:
                    mst["mb"] = k.bank(reserve=True)
                mb = mst["mb"]
                k.mm(P[mb][:, 0:MEM], [(ONESM[:, :], SQD[i][:, 0:MEM], [("SQD", i), ("ONES",)])], mb,
                     start=(c == 0), stop=(c == 7))

            def mem_apply():
                r = rstd_from(mst["mb"], MEM, 1.0)
                for c in range(8):
                    map_, mk = M32(c, (0, MEM))
                    oap, okeys = MN(c, (0, MEM))
                    col = CF[:, CF_MG + l * 8 + c:CF_MG + l * 8 + c + 1]
                    k.op("dve", lambda e, map_=map_, oap=oap, col=col: e.scalar_tensor_tensor(
                        out=oap, in0=map_, scalar=col, in1=R[r][:, 0:MEM], op0=ALU.mult, op1=ALU.mult),
                        reads=mk + [("R", r)] + CONST, writes=okeys)
            for hf, c8 in [(0, c_) for c_ in range(8)]:
                w = ws.acquire(("lin", "xa_wq", l, c8))
                for tb in (2 * hf, 2 * hf + 1):
                    lo, hi = tbr(tb)
                    b = k.bank()
                    k.mm(P[b][:, :], [(w.w(kk),) + tuple(_rk(XN(kk, (lo, hi)), w.key)) for kk in range(8)], b)
                    oap, okeys = Q(c8, (lo, hi))
                    evac_copy(oap, okeys, b, eng="act")
                    gi = mst["gi"]
                    mst["gi"] += 1
                    if 6 <= gi < 14:
                        mem_mm()
                    if 4 <= gi < 12:
                        mem_sq(gi - 4)
                    if gi == 14:
                        mem_apply()
                ws.release(w)
            for hf, c8 in [(1, c_) for c_ in range(4)]:
                w = ws.acquire(("lin", "xa_wq", l, c8))
                for tb in (2 * hf, 2 * hf + 1):
                    lo, hi = tbr(tb)
                    b = k.bank()
                    k.mm(P[b][:, :], [(w.w(kk),) + tuple(_rk(XN(kk, (lo, hi)), w.key)) for kk in range(8)], b)
                    oap, okeys = Q(c8, (lo, hi))
                    evac_copy(oap, okeys, b)
                ws.release(w)
            for c8 in range(8):
                w = ws.acquire(("lin", "xa_wkv", l, c8))
                b = k.bank()
                k.mm(P[b][:, 0:MEM], [(w.w(kk),) + tuple(_rk(MN(kk, (0, MEM)), w.key)) for kk in range(8)], b)
                oap, okeys = KT(c8, (0, MEM))
                evac_copy(oap, okeys, b, n=MEM)
                ws.release(w)
            for cb in range(4):
                wv0 = ws.acquire(("wv", l, cb, 0))
                wv1 = ws.acquire(("wv", l, cb, 1))
                for mt in range(2):
                    b = k.bank()
                    pairs = []
                    for kk in range(8):
                        map_, mk = MN(kk, (mt * 128, (mt + 1) * 128))
                        sl = wv0 if kk < 4 else wv1
                        pairs.append((map_, sl.t[:, (kk % 4) * 256:(kk % 4 + 1) * 256], mk + [sl.key]))
                    k.mm(P[b][:, 0:256], pairs, b)
                    oap, okeys = V(mt, (cb * 256, (cb + 1) * 256))
                    evac_copy(oap, okeys, b, n=256)
                ws.release(wv0)
                ws.release(wv1)
            for hf, c8 in [(1, c_) for c_ in range(4, 8)]:
                w = ws.acquire(("lin", "xa_wq", l, c8))
                for tb in (2 * hf, 2 * hf + 1):
                    lo, hi = tbr(tb)
                    b = k.bank()
                    k.mm(P[b][:, :], [(w.w(kk),) + tuple(_rk(XN(kk, (lo, hi)), w.key)) for kk in range(8)], b)
                    oap, okeys = Q(c8, (lo, hi))
                    evac_copy(oap, okeys, b)
                ws.release(w)
            for sb_ in range(NTB):
                lo, hi = tbr(sb_)
                for h in range(4):
                    et = ET[k.rot("ET", 3)]
                    for mt in range(2):
                        b = k.bank()
                        pairs = []
                        for dc in range(2):
                            kap, kk_ = KT(2 * h + dc, (mt * 128, (mt + 1) * 128))
                            qap, qk = Q(2 * h + dc, (lo, hi))
                            pairs.append((kap, qap, kk_ + qk))
                        k.mm(P[b][:, :], pairs, b)
                        eap, ek = et(mt, (0, TB))
                        k.op("act", lambda e, b=b, eap=eap: e.activation(out=eap, in_=P[b][:, :], func=AF.Exp, scale=1.0 / 16.0),
                             reads=[("P", b)], writes=ek)

                    def stage2(h=h, et=et, lo=lo, hi=hi):
                        b = k.bank()
                        pairs = []
                        for mt in range(2):
                            eap, ek = et(mt, (0, TB))
                            pairs.append((ONES1[:, :], eap, ek + CONST))
                        k.mm(P[b][:, :], pairs, b)
                        rd = RD[k.rot("RD", 2)]
                        rap, rk = rd((0, TB))
                        k.op("act", lambda e: e.activation(out=rap, in_=P[b][:, :], func=AF.Ln), reads=[("P", b)], writes=rk)
                        k.op("act", lambda e: e.activation(out=rap, in_=rap, func=AF.Exp, scale=-1.0), reads=rk, writes=rk)
                        for dc in range(2):
                            b2 = k.bank()
                            pairs = []
                            for mt in range(2):
                                vap, vk = V(mt, (h * 256 + dc * 128, h * 256 + (dc + 1) * 128))
                                eap, ek = et(mt, (0, TB))
                                pairs.append((vap, eap, vk + ek))
                            k.mm(P[b2][:, :], pairs, b2)
                            oap, okeys = O(2 * h + dc, (lo, hi))
                            k.op("dve", lambda e, b2=b2, oap=oap: e.tensor_tensor(out=oap, in0=P[b2][:, :], in1=rap, op=ALU.mult),
                                 reads=[("P", b2)] + rk, writes=okeys)
                    akeys = et(0, (0, TB))[1] + et(1, (0, TB))[1] + RD[0]((0, TB))[1] + RD[1]((0, TB))[1] \
                        + O(2 * h, (lo, hi))[1] + O(2 * h + 1, (lo, hi))[1]
                    k.defer(2, ("att",), stage2, akeys, akeys)

            def rhs(kk, tb):
                lo, hi = tbr(tb)
                return tuple_list(O(kk, (lo, hi)))
            out_proj(lambda c: [("lin", "xa_wo", l, c)], rhs, 8, l, n_post, False, nxt, ho2=lambda c, tl: HO2(c, tl))

        subs = []
        for l in range(L):
            subs += [("ffn", l, 0, 0, 1), ("mix", l, None, 2, 3), ("att", l, None, 4, 5), ("ffn", l, 1, 6, 7)]
        subs = subs[:nsub]
        prenorm_chain(0, 0, list(range(NTB)))
        for si, (kind, l, f, n_pre, n_post) in enumerate(subs):
            nxt = (subs[si + 1][1], subs[si + 1][3]) if si + 1 < len(subs) else None
            if kind == "ffn":
                ffn(l, f, n_post, nxt)
            elif kind == "mix":
                (even_mixer if l % 2 == 0 else odd_mixer)(l, n_post, nxt)
            else:
                attention(l, n_post, nxt)
        k.flush()
        assert k.cnt.get("yo", 0) == 16 * NTB
        k.wait_all("sp", ["yo"])

    def _rk(apk, key):
        return (apk[0], apk[1] + [key])

    def tuple_list(apk):
        return (apk[0], list(apk[1]))

    kd = K(None)
    program(kd)
    sched = list(kd.ws.log)
    k = K(sched)
    program(k)

    semh = {name: nc.alloc_semaphore("s_" + name) for name in k.cnt}

    def replay(eng_name, e):
        for waits, fn, inc in k.q[eng_name]:
            for s_, v in waits:
                e.wait_ge(semh[s_], v)
            if fn is None:
                continue
            ins = fn(e)
            if inc is not None:
                ins.then_inc(semh[inc[0]], inc[1])

    with nc.Block() as block:
        @block.sync
        def _(e):
            replay("sp", e)

        @block.gpsimd
        def _(e):
            replay("pool", e)

        @block.scalar
        def _(e):
            replay("act", e)

        @block.vector
        def _(e):
            replay("dve", e)

        @block.tensor
        def _(e):
            replay("pe", e)
    return nc


_NC_CACHE = {}


def run(inputs, nsub=8, cores=NCORES, trace=False):
    inp = {k_: np.asarray(v, dtype=np.float32) for k_, v in inputs.items()}
    wflat = pack_weights(inp)
    cf, cb = pack_consts(inp)
    alt = (((-1.0) ** np.arange(S)) / np.sqrt(float(S))).astype(np.float32).reshape(1, S)
    if nsub not in _NC_CACHE:
        _NC_CACHE[nsub] = build(nsub)
    nc = _NC_CACHE[nsub]
    in_maps = []
    for b in range(cores):
        in_maps.append({
            "xT": np.ascontiguousarray(inp["x"][b].T),
            "memT": np.ascontiguousarray(inp["mem"][b].T),
            "wflat": wflat, "cf": cf, "cb": cb, "alt": alt,
        })
    res = run_bass_kernel_spmd(nc, in_maps, core_ids=list(range(cores)), trace=trace)
    out = np.stack([np.ascontiguousarray(res.results[b]["yT"].T) for b in range(cores)], axis=0)
    return out.astype(np.float32), res


def kernel(**inputs):
    out, _ = run(inputs)
    return out
```

```python
import numpy as np
import concourse.bass as bass
import concourse.mybir as mybir
from concourse.bass_utils import run_bass_kernel_spmd

F32 = mybir.dt.float32
BF16 = mybir.dt.bfloat16
AF = mybir.ActivationFunctionType
ALU = mybir.AluOpType

D = 1024
S = 2048
DFF = 2816
NJ = DFF // 128
MEM = 256
L = 2
TB = 512
NTB = 4
EPS = 1e-6
NSLOT = 8
SLOT_ELEMS = 1024
NCORES = 8

SB_BASE = 16512
SB_TOP = 229344
OFF_X = 0
OFF_ARENA = OFF_X + 65536
ARENA_BYTES = 110592
OFF_SLOTS = OFF_ARENA + ARENA_BYTES
OFF_SQ = OFF_SLOTS + NSLOT * 2048
OFF_R = OFF_SQ + 7 * 1024
OFF_S = OFF_R + 2 * 2048
OFF_CF = OFF_S + 2 * 2048
NCF = 128 + 16 + 124 + 4 + 4 + 4 + 24
CF_G, CF_MG, CF_DW, CF_DWB, CF_GNG, CF_GNB, CF_OD = 0, 128, 144, 268, 272, 276, 280
OFF_CB = OFF_CF + NCF * 4
NCB = 512 + 256 + 128
CB_FW, CB_CSC, CB_ID = 0, 512, 768
OFF_ONES = OFF_CB + NCB * 2
OFF_END = OFF_ONES + 256 + 256 + 512
assert SB_BASE + OFF_END <= SB_TOP, (SB_BASE + OFF_END, SB_TOP)

A_XN = 0
A_H = 32768
A_HO = 77824


def _pkc(mat):
    nk = mat.shape[0] // 128
    return np.ascontiguousarray(mat.reshape(nk, 128, mat.shape[1]).transpose(1, 0, 2)).reshape(128, -1)


def _dft_tables():
    n = np.arange(S, dtype=np.int64)
    m = (n[:, None] * n[None, :]) % S
    ang = 2.0 * np.pi * m.astype(np.float64) / S
    cs = (np.cos(ang) / np.sqrt(S)).astype(np.float32)
    ns = (-np.sin(ang) / np.sqrt(S)).astype(np.float32)
    return cs, ns


def panel_specs():
    specs = []
    for l in range(L):
        for f in range(2):
            for j in range(NJ):
                for nm, gu in (("ffn_w_gate", "g"), ("ffn_w_up", "u")):
                    specs.append((("gu", l, f, j, gu), 1024,
                                  lambda inp, c, nm=nm, l=l, f=f, j=j: _pkc(inp[nm][l, f][:, j * 128:(j + 1) * 128])))
            for c in range(8):
                for pc, (r0, r1) in enumerate(((0, 1024), (1024, 2048), (2048, 2816))):
                    specs.append((("d", l, f, c, pc), (r1 - r0),
                                  lambda inp, cc, l=l, f=f, c=c, r0=r0, r1=r1:
                                  _pkc(inp["ffn_w_down"][l, f][r0:r1, c * 128:(c + 1) * 128])))
        for nm, ncol in (("xa_wq", 8), ("xa_wkv", 8), ("xa_wo", 8)):
            for c in range(ncol):
                specs.append((("lin", nm, l, c), 1024,
                              lambda inp, cc, nm=nm, l=l, c=c: _pkc(inp[nm][l][:, c * 128:(c + 1) * 128])))
        for cb in range(4):
            for kh in range(2):
                specs.append((("wv", l, cb, kh), 1024,
                              lambda inp, cc, l=l, cb=cb, kh=kh:
                              _pkc(inp["xa_wkv"][l][kh * 512:(kh + 1) * 512, 1024 + cb * 256:1024 + (cb + 1) * 256])))
    for nm, ncol in (("ev_w_in", 12), ("ev_w_out", 8), ("od_w_in", 24), ("od_w_out", 8)):
        for c in range(ncol):
            specs.append((("lin", nm, 0, c), 1024,
                          lambda inp, cc, nm=nm, c=c: _pkc(inp[nm][0][:, c * 128:(c + 1) * 128])))
    for which in range(2):
        for kb in range(4):
            for ttp in range(4):
                specs.append((("dft", which, kb, ttp), 1024,
                              lambda inp, cc, which=which, kb=kb, ttp=ttp:
                              _pkc(cc["dft"][which][ttp * 256:(ttp + 1) * 256, kb * 512:(kb + 1) * 512])))
    return specs


_SPECS = panel_specs()
PANEL = {}
_off = 0
for _k, _n, _g in _SPECS:
    PANEL[_k] = (_off, _n)
    _off += 128 * _n
WFLAT_ELEMS = _off


def pack_weights(inp):
    cache = {"dft": _dft_tables()}
    flat = np.empty(WFLAT_ELEMS, dtype=np.float32)
    for key, n, g in _SPECS:
        off, _ = PANEL[key]
        a = g(inp, cache)
        assert a.shape == (128, n), (key, a.shape)
        flat[off:off + 128 * n] = a.reshape(-1)
    return flat


def pack_consts(inp):
    cf = np.zeros((128, NCF), np.float32)
    ng = inp["norm_g"]
    for l in range(L):
        for n in range(8):
            for c in range(8):
                cf[:, CF_G + (l * 8 + n) * 8 + c] = ng[l, n, c * 128:(c + 1) * 128]
        for c in range(8):
            cf[:, CF_MG + l * 8 + c] = inp["mem_norm_g"][l, c * 128:(c + 1) * 128]
    for i in range(4):
        for j in range(31):
            cf[:, CF_DW + i * 31 + j] = inp["ev_dw_w"][0, j, i * 128:(i + 1) * 128]
        cf[:, CF_DWB + i] = inp["ev_dw_b"][0, i * 128:(i + 1) * 128]
        cf[:, CF_GNG + i] = inp["ev_gn_g"][0, i * 128:(i + 1) * 128]
        cf[:, CF_GNB + i] = inp["ev_gn_b"][0, i * 128:(i + 1) * 128]
    for i in range(8):
        for j in range(3):
            cf[:, CF_OD + i * 3 + j] = inp["od_conv_w"][0, j, i * 128:(i + 1) * 128]
    cb = np.zeros((128, NCB), np.float32)
    for g in range(4):
        cb[:, CB_FW + g * 128:CB_FW + (g + 1) * 128] = inp["ev_fourier_w"][0, g]
    c = np.arange(128)
    ang = 2.0 * np.pi * ((c[:, None] * c[None, :]) % 128) / 128.0
    cb[:, CB_CSC:CB_CSC + 128] = np.cos(ang) / np.sqrt(128.0)
    cb[:, CB_CSC + 128:CB_CSC + 256] = np.sin(ang) / np.sqrt(128.0)
    cb[:, CB_ID:CB_ID + 128] = np.eye(128)
    return cf, cb


class K:
    ENGS = ("pe", "act", "dve", "pool", "sp")

    def __init__(self, sched=None):
        self.q = {e: [] for e in self.ENGS}
        self.cnt = {}
        self.seen = {}
        self.tr = {}
        self.deferred = []
        self._in_def = False
        self._bank = 0
        self.reserved = set()
        self._rot = {}
        self.load = {"act": 0, "dve": 0}
        self.ws = WS(self, sched)

    def _collect(self, eng, reads, writes):
        need = {}
        for k_ in reads:
            e = self.tr.get(k_)
            if e and e[0]:
                s, v = e[0]
                if need.get(s, 0) < v:
                    need[s] = v
        for k_ in writes:
            e = self.tr.get(k_)
            if e:
                if e[0]:
                    s, v = e[0]
                    if need.get(s, 0) < v:
                        need[s] = v
                for s, v in e[1].items():
                    if need.get(s, 0) < v:
                        need[s] = v
        out = []
        for s, v in need.items():
            if eng == "pe" and s == "pe":
                continue
            if self.seen.get((eng, s), 0) >= v:
                continue
            self.seen[(eng, s)] = v
            out.append((s, v))
        return out

    def _commit(self, sem, val, reads, writes):
        for k_ in writes:
            self.tr[k_] = [(sem, val), {}]
        for k_ in reads:
            e = self.tr.get(k_)
            if e is None:
                e = self.tr[k_] = [None, {}]
            if e[1].get(sem, 0) < val:
                e[1][sem] = val

    def op(self, eng, fn, reads=(), writes=()):
        self._check(reads, writes)
        waits = self._collect(eng, reads, writes)
        self.cnt[eng] = self.cnt.get(eng, 0) + 1
        self.q[eng].append((waits, fn, (eng, 1)))
        self._commit(eng, self.cnt[eng], reads, writes)

    def dma(self, eng, fn, sem, reads=(), writes=()):
        self._check(reads, writes)
        waits = self._collect(eng, reads, writes)
        self.cnt[sem] = self.cnt.get(sem, 0) + 16
        self.q[eng].append((waits, fn, (sem, 16)))
        self._commit(sem, self.cnt[sem], reads, writes)

    def wait_all(self, eng, sems):
        self.q[eng].append(([(s, self.cnt[s]) for s in sems if self.cnt.get(s, 0) > 0], None, None))

    def mm(self, out, pairs, bank, start=True, stop=True):
        n = len(pairs)
        allreads = []
        bk = ("P", bank)
        self._check([k_ for p_ in pairs for k_ in p_[2]], [bk])
        for i, (l_, r_, rd) in enumerate(pairs):
            last = i == n - 1
            waits = self._collect("pe", rd, [bk] if (i == 0 and start) else [])
            st = bool(start and i == 0)
            sp = bool(stop and last)
            fn = (lambda e, l_=l_, r_=r_, st=st, sp=sp: e.matmul(out, lhsT=l_, rhs=r_, start=st, stop=sp))
            inc = None
            if last:
                self.cnt["pe"] = self.cnt.get("pe", 0) + 1
                inc = ("pe", 1)
            self.q["pe"].append((waits, fn, inc))
            allreads.extend(rd)
        self._commit("pe", self.cnt["pe"], allreads, [bk])
        self._tick()

    def defer(self, n, tag, fn, rkeys=(), wkeys=()):
        self.deferred.append([n, tag, fn, set(rkeys) | set(wkeys), set(wkeys)])

    def _check(self, reads, writes):
        if self._in_def:
            return
        while self.deferred:
            hit = -1
            for i, d in enumerate(self.deferred):
                rw, w = d[3], d[4]
                if not rw:
                    continue
                if any(k_ in rw for k_ in writes) or any(k_ in w for k_ in reads):
                    hit = i
            if hit < 0:
                return
            for _ in range(hit + 1):
                self._run_head()

    def _run_head(self):
        n, tag, fn, _rw, _w = self.deferred.pop(0)
        prev = self._in_def
        self._in_def = True
        fn()
        self._in_def = prev

    def _tick(self):
        if self._in_def:
            return
        for d in self.deferred:
            d[0] -= 1
        while self.deferred and self.deferred[0][0] <= 0:
            self._run_head()

    def need(self, tag):
        idx = [i for i, d in enumerate(self.deferred) if d[1] == tag]
        if not idx:
            return
        last = idx[-1]
        target = self.deferred[last]
        while any(d is target for d in self.deferred):
            self._run_head()

    def flush(self):
        while self.deferred:
            self._run_head()

    def bank(self, reserve=False):
        for _ in range(8):
            b = self._bank
            self._bank = (self._bank + 1) % 8
            if b not in self.reserved:
                if reserve:
                    self.reserved.add(b)
                return b
        raise RuntimeError("no free PSUM bank")

    def rot(self, name, n):
        i = self._rot.get(name, 0)
        self._rot[name] = (i + 1) % n
        return i

    def evac_eng(self, elems):
        e = "act" if self.load["act"] <= self.load["dve"] else "dve"
        self.load[e] += elems
        return e


class Slot:
    def __init__(self, pos, idx, t):
        self.pos, self.idx, self.t = pos, idx, t
        self.key = ("W", idx)

    def w(self, kk):
        return self.t[:, kk * 128:(kk + 1) * 128]


class WS:
    def __init__(self, k, sched):
        self.k, self.sched = k, sched
        self.log = []
        self.pos = 0
        self.issued = 0
        self.released = 0
        self.slots = None
        self.wflat = None

    def _issue(self):
        i = self.issued
        idx = i % NSLOT
        off, n = PANEL[self.sched[i]]
        src = self.wflat[off:off + 128 * n].rearrange("(p n) -> p n", p=128)
        dst = self.slots[idx][:, 0:n]
        self.k.dma("pool", lambda e, dst=dst, src=src: e.dma_start(out=dst, in_=src), "w%d" % idx,
                   reads=([("X", c, 0) for c in range(4)] if i == 0 else ()), writes=[("W", idx)])
        self.issued += 1

    def acquire(self, pkey):
        self.log.append(pkey)
        pos = self.pos
        self.pos += 1
        if self.sched is None:
            return Slot(pos, pos % NSLOT, self.slots[pos % NSLOT])
        assert self.sched[pos] == pkey, (pos, self.sched[pos], pkey)
        while self.issued < min(len(self.sched), max(NSLOT, pos + 1)) and self.issued < self.released + NSLOT:
            self._issue()
        assert self.issued > pos
        return Slot(pos, pos % NSLOT, self.slots[pos % NSLOT])

    def release(self, slot):
        assert slot.pos == self.released, (slot.pos, self.released)
        self.released += 1
        if self.sched is None:
            return
        while self.issued < len(self.sched) and self.issued < self.released + NSLOT:
            self._issue()


class AT:
    def __init__(self, nc, name, off, shape, dt):
        self.esz = 2 if dt == BF16 else 4
        self.off = off
        self.shape = list(shape)
        nbytes = int(np.prod(shape)) * self.esz
        assert off + nbytes <= ARENA_BYTES, (name, off, nbytes)
        self.h = nc.alloc_sbuf_tensor_at(name, [128] + list(shape), dt, offset=SB_BASE + OFF_ARENA + off)
        self.strides = [int(np.prod(shape[i + 1:])) for i in range(len(shape))]

    def __call__(self, *idx):
        assert len(idx) == len(self.shape)
        sl = [slice(None)]
        for i in idx:
            sl.append(i if isinstance(i, int) else slice(i[0], i[1]))
        ap = self.h[tuple(sl)]
        lead = idx[:-1]
        last = idx[-1]
        lo, hi = (last, last + 1) if isinstance(last, int) else last
        combos = [0]
        for d, i in enumerate(lead):
            rng = [i] if isinstance(i, int) else range(i[0], i[1])
            combos = [c + r * self.strides[d] for c in combos for r in rng]
        keys = set()
        for c in combos:
            b0 = (self.off + (c + lo) * self.esz) // 1024
            b1 = (self.off + (c + hi) * self.esz - 1) // 1024
            for b in range(b0, b1 + 1):
                keys.add(("A", b))
        return ap, list(keys)


def build(nsub=8):
    nc = bass.Bass("TRN2", target_bir_lowering=False)
    xT = nc.dram_tensor("xT", [D, S], F32, kind="ExternalInput").ap()
    memT = nc.dram_tensor("memT", [D, MEM], F32, kind="ExternalInput").ap()
    wflat = nc.dram_tensor("wflat", [WFLAT_ELEMS], F32, kind="ExternalInput").ap()
    cfd = nc.dram_tensor("cf", [128, NCF], F32, kind="ExternalInput").ap()
    cbd = nc.dram_tensor("cb", [128, NCB], F32, kind="ExternalInput").ap()
    altd = nc.dram_tensor("alt", [1, S], F32, kind="ExternalInput").ap()
    yT = nc.dram_tensor("yT", [D, S], F32, kind="ExternalOutput").ap()

    def sb(name, off, shape, dt):
        return nc.alloc_sbuf_tensor_at(name, [128] + list(shape), dt, offset=SB_BASE + off)

    X = sb("X", OFF_X, [8, S], F32)
    SLOTS = [sb("slot%d" % i, OFF_SLOTS + i * 2048, [SLOT_ELEMS], BF16) for i in range(NSLOT)]
    SQD = [sb("sqd%d" % i, OFF_SQ + i * 1024, [TB], BF16) for i in range(3)]
    SQP = sb("sqp", OFF_SQ + 3 * 1024, [4, TB], BF16)
    R = [sb("r%d" % i, OFF_R + i * 2048, [TB], F32) for i in range(2)]
    SS = [sb("s%d" % i, OFF_S + i * 2048, [TB], F32) for i in range(2)]
    CF = sb("cf_sb", OFF_CF, [NCF], F32)
    CB = sb("cb_sb", OFF_CB, [NCB], BF16)
    ONESM = sb("onesm", OFF_ONES, [128], BF16)
    ONES1 = sb("ones1", OFF_ONES + 256, [128], BF16)
    ONESF = sb("onesf", OFF_ONES + 512, [128], F32)
    P = [nc.alloc_psum_tensor("ps%d" % i, [128, TB], F32) for i in range(8)]

    XN = AT(nc, "XN", A_XN, [8, S], BF16)
    H = AT(nc, "H", A_H, [NJ, 1024], BF16)
    HO = AT(nc, "HO", A_HO, [8, 1024], F32)
    UF = AT(nc, "UF", 32768, [4, S], BF16)
    YA = AT(nc, "YA", 32768, [4, S], BF16)
    CC = AT(nc, "CC", 49152, [4, S + 32], BF16)
    FB = AT(nc, "FB", 49152 + 16640, [4, TB], BF16)
    AB = AT(nc, "AB", 0, [9, 4, 256], BF16)
    USY = AT(nc, "USY", A_HO, [4, 1152], BF16)
    UAS = AT(nc, "UAS", A_HO + 9216, [4, 1024], BF16)
    ALT = AT(nc, "ALT", A_HO + 17408, [S], BF16)
    YB = AT(nc, "YB", 0, [4, S], BF16)
    DG = [AT(nc, "DG0", 69888, [31, 128], BF16), AT(nc, "DG1", A_HO + 24576, [31, 128], BF16)]
    GS = {nm: [AT(nc, "gs_%s%d" % (nm, i), A_HO + (j * 3 + i) * 2048, [TB], F32) for i in range(3)]
          for j, nm in enumerate(("c1", "d", "d2", "sd"))}
    GSB = {"c1b": [AT(nc, "gsb_c1b%d" % i, A_HO + (3 * 3 + i) * 2048, [TB], BF16) for i in range(3)],
           "d2b": [AT(nc, "gsb_d2b%d" % i, A_HO + (2 * 3 + i) * 2048, [TB], BF16) for i in range(3)]}
    YO = AT(nc, "YO", 32768, [8, S], BF16)
    CV = [AT(nc, "CV%d" % i, 65536 + i * 8448, [S + 64], F32) for i in range(2)]
    TT = [AT(nc, "TT%d" % i, 65536 + 2 * 8448 + i * 8192, [S], F32) for i in range(2)]
    O = AT(nc, "O", 0, [8, S], BF16)
    HO2 = AT(nc, "HO2", 32768, [8, 1024], F32)
    Q = AT(nc, "Q", 32768, [8, S], BF16)
    M32 = AT(nc, "M32", 65536, [8, MEM], F32)
    MN = AT(nc, "MN", 73728, [8, MEM], BF16)
    KT = AT(nc, "KT", 65536, [8, MEM], BF16)
    V = AT(nc, "V", 69632, [2, D], BF16)
    ET = [AT(nc, "ET%d" % i, 86016 + i * 2048, [2, TB], BF16) for i in range(3)]
    RD = [AT(nc, "RD%d" % i, 92160 + i * 2048, [TB], F32) for i in range(2)]

    def tbr(tb):
        return (tb * TB, (tb + 1) * TB)

    def G(l, n, c):
        i = CF_G + (l * 8 + n) * 8 + c
        return CF[:, i:i + 1]

    def program(k):
        ws = k.ws
        ws.slots = SLOTS
        ws.wflat = wflat
        xk = lambda c, tb: ("X", c, tb)

        k.dma("sp", lambda e: e.dma_start(out=CF[:, :], in_=cfd), "c0", writes=[("CF",)])
        k.dma("pool", lambda e: e.dma_start(out=CB[:, :], in_=cbd), "c1", writes=[("CB",)])
        xv = xT.rearrange("(c p) t -> p c t", p=128)
        for tb in range(NTB):
            lo, hi = tbr(tb)
            for ch in range(2):
                k.dma("sp", lambda e, lo=lo, hi=hi, ch=ch: e.dma_start(out=X[:, 4 * ch:4 * ch + 4, lo:hi],
                                                                       in_=xv[:, 4 * ch:4 * ch + 4, lo:hi]), "xl%d_%d" % (tb, ch),
                      writes=[xk(c, tb) for c in range(4 * ch, 4 * ch + 4)])
        k.op("dve", lambda e: e.memset(ONESM[:, :], 1.0 / 1024.0), writes=[("ONES",)])
        k.op("dve", lambda e: e.memset(ONES1[:, :], 1.0), writes=[("ONES",)])
        k.op("dve", lambda e: e.memset(ONESF[:, :], 1.0 / 128.0), writes=[("ONES",)])
        CONST = [("CF",), ("CB",), ("ONES",)]

        def rstd_from(bank, n, sc):
            r = k.rot("R", 2)
            k.op("act", lambda e: e.activation(out=R[r][:, 0:n], in_=P[bank][:, 0:n], func=AF.Ln, bias=sc * EPS, scale=sc),
                 reads=[("P", bank)], writes=[("R", r)])
            k.op("act", lambda e: e.activation(out=R[r][:, 0:n], in_=R[r][:, 0:n], func=AF.Exp, scale=-0.5),
                 reads=[("R", r)], writes=[("R", r)])
            k.reserved.discard(bank)
            return r

        def chain_keys(tbs, with_ho):
            xks = [xk(c, tb) for c in range(8) for tb in tbs]
            xnk = []
            for tb in tbs:
                xnk += XN((0, 8), tbr(tb))[1]
            hok = HO((0, 8), (0, 1024))[1] if with_ho else []
            return hok + xks, hok + xks + xnk

        def run_seq(steps):
            while any(d[1] == ("seq",) for d in k.deferred):
                k.need(("seq",))
            n_ = len(steps)
            suf_r = [set() for _ in range(n_ + 1)]
            suf_w = [set() for _ in range(n_ + 1)]
            for i in range(n_ - 1, -1, -1):
                suf_r[i] = suf_r[i + 1] | set(steps[i][1].rk)
                suf_w[i] = suf_w[i + 1] | set(steps[i][1].wk)

            def go(i):
                if i >= n_:
                    return
                d, fn = steps[i]

                def wrapped():
                    fn()
                    go(i + 1)
                if d <= 0:
                    wrapped()
                else:
                    k.defer(d, ("seq",), wrapped, suf_r[i], suf_w[i])
            go(0)

        def keyed(fn, rk=(), wk=()):
            fn.rk = list(rk)
            fn.wk = list(wk)
            return fn

        def both(*fns):
            def fn():
                for f_ in fns:
                    f_()
            return keyed(fn, [k_ for f_ in fns for k_ in f_.rk], [k_ for f_ in fns for k_ in f_.wk])

        def prenorm_steps(l, n, tb):
            lo, hi = tbr(tb)
            st = {}

            def sa():
                k.op("act", lambda e: e.activation(out=SQP[:, :, :], in_=X[:, 0:4, lo:hi], func=AF.Square),
                     reads=[xk(c, tb) for c in range(4)], writes=[("SQP",)])

            def sb_():
                st["b"] = bnk = k.bank(reserve=True)
                for c in range(4):
                    k.mm(P[bnk][:, :], [(ONESM[:, :], SQP[:, c, :], [("SQP",), ("ONES",)])], bnk, start=(c == 0), stop=False)
                k.op("act", lambda e: e.activation(out=SQP[:, :, :], in_=X[:, 4:8, lo:hi], func=AF.Square),
                     reads=[xk(c, tb) for c in range(4, 8)], writes=[("SQP",)])

            def xn_apply(c):
                r = st["r"]
                oap, okeys = XN(c, (lo, hi))
                k.op("dve", lambda e, c=c, oap=oap: e.scalar_tensor_tensor(
                    out=oap, in0=X[:, c, lo:hi], scalar=G(l, n, c), in1=R[r][:, :], op0=ALU.mult, op1=ALU.mult),
                    reads=[xk(c, tb), ("R", r)] + CONST, writes=okeys)

            def sc1():
                bnk = st["b"]
                for c in range(4):
                    k.mm(P[bnk][:, :], [(ONESM[:, :], SQP[:, c, :], [("SQP",), ("ONES",)])], bnk, start=False, stop=(c == 3))
                st["r"] = rstd_from(bnk, TB, 1.0)
                for c in range(4):
                    xn_apply(c)

            def sc2():
                for c in range(4, 8):
                    xn_apply(c)
            keyed(sa, [xk(c, tb) for c in range(4)])
            keyed(sb_, [xk(c, tb) for c in range(4, 8)])
            keyed(sc1, [xk(c, tb) for c in range(4)], XN((0, 4), (lo, hi))[1])
            keyed(sc2, [xk(c, tb) for c in range(4, 8)], XN((4, 8), (lo, hi))[1])
            return sa, sb_, sc1, sc2

        def prenorm_chain(l, n, tbs):
            steps = []
            for tb in tbs:
                sa, sb_, sc1, sc2 = prenorm_steps(l, n, tb)
                steps += [(0 if not steps else 1, sa), (3, sb_), (3, sc1), (1, sc2)]
            run_seq(steps)

        HO_PENDING = []

        class HalfOut:
            def __init__(self, l, n_post, half, tbs, hoap=None):
                self.l, self.n, self.half, self.tbs = l, n_post, half, tbs
                self.hoap = hoap or (lambda c, tl: HO(c, tl))
                self.pst = {}
                self.pend = []

            def evac(self, c, tb, bank):
                for o_ in list(HO_PENDING):
                    if o_ is not self:
                        o_.stat_mm(0)
                if self not in HO_PENDING:
                    HO_PENDING.append(self)
                tl = ((tb % 2) * TB, (tb % 2 + 1) * TB)
                oap, okeys = self.hoap(c, tl)
                gcol = G(self.l, self.n, c)
                k.op("act", lambda e: e.activation(out=oap, in_=P[bank][:, :], func=AF.Copy, scale=gcol),
                     reads=[("P", bank)] + CONST, writes=okeys)
                i = k.rot("SQD", 3)
                k.op("act", lambda e: e.activation(out=SQD[i][:, :], in_=P[bank][:, :], func=AF.Square),
                     reads=[("P", bank)], writes=[("SQD", i)])
                self.pend.append((tb, c, i))
                self.stat_mm(2)

            def stat_mm(self, keep):
                if keep == 0 and self in HO_PENDING:
                    HO_PENDING.remove(self)
                while len(self.pend) > keep:
                    tb, c, i = self.pend.pop(0)
                    if tb not in self.pst:
                        self.pst[tb] = k.bank(reserve=True)
                    bnk = self.pst[tb]
                    k.mm(P[bnk][:, :], [(ONESM[:, :], SQD[i][:, :], [("SQD", i), ("ONES",)])], bnk,
                         start=(c == 0), stop=(c == 7))

            def finish(self, nxt, hf, cpb=1):
                fin_stats = keyed(lambda: self.stat_mm(0), [("SQD", i_) for i_ in range(3)], [])
                ta, tb_ = self.tbs[0], self.tbs[-1]
                sc = 4.0 if self.half else 1.0
                rr = {}
                if hf == 1:
                    cpb = 1

                def add_eng_for(tb):
                    if nxt is None:
                        return "dve"
                    if hf == 0:
                        return "pool"
                    return "pool" if tb == ta else "dve"

                def A(tb, c0):
                    def fn():
                        lo, hi = tbr(tb)
                        tl = ((tb % 2) * TB, (tb % 2 + 1) * TB)
                        if c0 == 0:
                            rr[tb] = rstd_from(self.pst[tb], TB, sc)
                        r = rr[tb]
                        for c in range(c0, c0 + cpb):
                            hap, hkeys = self.hoap(c, tl)
                            k.op("dve", lambda e, hap=hap: e.tensor_tensor(out=hap, in0=hap, in1=R[r][:, :], op=ALU.mult),
                                 reads=hkeys + [("R", r)], writes=hkeys)
                            k.op(add_eng_for(tb), lambda e, hap=hap, c=c: e.tensor_tensor(out=X[:, c, lo:hi], in0=X[:, c, lo:hi], in1=hap,
                                                                                         op=ALU.add),
                                 reads=hkeys + [xk(c, tb)], writes=[xk(c, tb)])
                    tl_ = ((tb % 2) * TB, (tb % 2 + 1) * TB)
                    ks = [xk(c, tb) for c in range(c0, c0 + cpb)]
                    for c in range(c0, c0 + cpb):
                        ks += self.hoap(c, tl_)[1]
                    return keyed(fn, ks, ks)
                cs = list(range(0, 8, cpb))
                steps = [(2, fin_stats)] + [(1, A(ta, c0)) for c0 in cs]
                if len(self.tbs) == 1:
                    assert nxt is None
                    steps += [(0, keyed(lambda: store_out(ta), [xk(c, ta) for c in range(8)]))]
                elif nxt is None:
                    steps += [(0, keyed(lambda: store_out(ta), [xk(c, ta) for c in range(8)]))]
                    steps += [(1, A(tb_, c0)) for c0 in cs]
                    steps += [(0, keyed(lambda: store_out(tb_), [xk(c, tb_) for c in range(8)]))]
                elif cpb == 1:
                    sa0, sb0, sc10, sc20 = prenorm_steps(nxt[0], nxt[1], ta)
                    sa1, sb1, sc11, sc21 = prenorm_steps(nxt[0], nxt[1], tb_)
                    steps += [(1, both(A(tb_, 0), sa0)), (1, A(tb_, 1)), (1, A(tb_, 2)), (1, both(A(tb_, 3), sb0)),
                              (1, A(tb_, 4)), (1, A(tb_, 5)), (1, both(A(tb_, 6), sc10)), (1, both(A(tb_, 7), sc20)),
                              (2, sa1), (3, sb1), (3, sc11), (1, sc21)]
                else:
                    assert cpb == 2
                    def A2(c0):
                        def fn():
                            for tb in (ta, tb_):
                                if c0 == 0:
                                    rr[tb] = rstd_from(self.pst[tb], TB, sc)
                            for c in (c0, c0 + 1):
                                for tb in (ta, tb_):
                                    lo, hi = tbr(tb)
                                    tl = ((tb % 2) * TB, (tb % 2 + 1) * TB)
                                    r = rr[tb]
                                    hap, hkeys = self.hoap(c, tl)
                                    k.op("dve", lambda e, hap=hap, r=r: e.tensor_tensor(out=hap, in0=hap, in1=R[r][:, :], op=ALU.mult),
                                         reads=hkeys + [("R", r)], writes=hkeys)
                                    k.op("pool" if (tb == ta and hf == 0) else "dve",
                                         lambda e, hap=hap, c=c, lo=lo, hi=hi: e.tensor_tensor(
                                             out=X[:, c, lo:hi], in0=X[:, c, lo:hi], in1=hap, op=ALU.add),
                                         reads=hkeys + [xk(c, tb)], writes=[xk(c, tb)])
                        ks = []
                        for c in (c0, c0 + 1):
                            for tb in (ta, tb_):
                                ks += [xk(c, tb)] + self.hoap(c, ((tb % 2) * TB, (tb % 2 + 1) * TB))[1]
                        return keyed(fn, ks, ks)
                    sa0, sb0, sc10, sc20 = prenorm_steps(nxt[0], nxt[1], ta)
                    sa1, sb1, sc11, sc21 = prenorm_steps(nxt[0], nxt[1], tb_)
                    steps = [(2, fin_stats), (1, A2(0)), (1, A2(2)), (1, A2(4)), (1, A2(6)),
                             (9, sa0), (11, sb0), (3, sc10), (1, both(sc20, sa1)), (3, sb1), (3, sc11), (1, sc21)]
                run_seq(steps)

        def store_out(tb):
            lo, hi = tbr(tb)
            yv = yT.rearrange("(c p) t -> p c t", p=128)
            k.dma("sp", lambda e: e.dma_start(out=yv[:, :, lo:hi], in_=X[:, :, lo:hi]), "yo",
                  reads=[("X", c, tb) for c in range(8)])

        def evac_copy(oap, okeys, bank, n=TB, eng=None):
            eng = eng or k.evac_eng(n)
            if eng == "act":
                k.op("act", lambda e: e.activation(out=oap, in_=P[bank][:, 0:n], func=AF.Copy),
                     reads=[("P", bank)], writes=okeys)
            else:
                k.op("dve", lambda e: e.tensor_copy(out=oap, in_=P[bank][:, 0:n]), reads=[("P", bank)], writes=okeys)

        def out_proj(pkeys_for_c, rhs, nk, l, n_post, half, nxt, ho2=None):
            for hf in range(2):
                tbs = (2 * hf, 2 * hf + 1)
                held = []
                ho = HalfOut(l, n_post, half, tbs, hoap=(ho2 if hf == 1 else None))

                def emit_held():
                    c_, tb_, b_ = held.pop(0)
                    ho.evac(c_, tb_, b_)
                    k.reserved.discard(b_)
                for c in range(8):
                    slots = [ws.acquire(pk) for pk in pkeys_for_c(c)]
                    for tb in tbs:
                        b = k.bank(reserve=True)
                        pairs = []
                        for kk in range(nk):
                            rap, rkeys = rhs(kk, tb)
                            sl = slots[kk // 8]
                            pairs.append((sl.w(kk % 8), rap, rkeys + [sl.key]))
                        k.mm(P[b][:, :], pairs, b)
                        held.append((c, tb, b))
                        if len(held) > 3:
                            emit_held()
                    for sl in slots:
                        ws.release(sl)
                while held:
                    emit_held()
                ho.finish(nxt, hf, cpb=2)

        def need_xn(tb):
            pass

        def ffn(l, f, n_post, nxt):
            for hf in range(2):
                tbs = (2 * hf, 2 * hf + 1)
                for tb in tbs:
                    need_xn(tb)
                for j in range(NJ):
                    wg = ws.acquire(("gu", l, f, j, "g"))
                    wu = ws.acquire(("gu", l, f, j, "u"))
                    for tb in tbs:
                        lo, hi = tbr(tb)
                        tl = ((tb % 2) * TB, (tb % 2 + 1) * TB)
                        bg = k.bank()
                        pairs = []
                        for kk in range(8):
                            rap, rkeys = XN(kk, (lo, hi))
                            pairs.append((wg.w(kk), rap, rkeys + [wg.key]))
                        k.mm(P[bg][:, :], pairs, bg)
                        s = k.rot("S", 2)
                        k.op("act", lambda e, bg=bg, s=s: e.activation(out=SS[s][:, :], in_=P[bg][:, :], func=AF.Silu),
                             reads=[("P", bg)], writes=[("S", s)])
                        bu = k.bank()
                        pairs = []
                        for kk in range(8):
                            rap, rkeys = XN(kk, (lo, hi))
                            pairs.append((wu.w(kk), rap, rkeys + [wu.key]))
                        k.mm(P[bu][:, :], pairs, bu)
                        hap, hkeys = H(j, tl)
                        k.op("dve", lambda e, bu=bu, s=s, hap=hap: e.tensor_tensor(out=hap, in0=P[bu][:, :], in1=SS[s][:, :],
                                                                                    op=ALU.mult),
                             reads=[("P", bu), ("S", s)], writes=hkeys)
                    ws.release(wg)
                    ws.release(wu)
                if nxt is None and hf == 1:
                    for tb in tbs:
                        tl = ((tb % 2) * TB, (tb % 2 + 1) * TB)
                        ho = HalfOut(l, n_post, True, (tb,))
                        for c in range(8):
                            slots = [ws.acquire(("d", l, f, c, pc)) for pc in range(3)]
                            b = k.bank()
                            pairs = []
                            for j in range(NJ):
                                rap, rkeys = H(j, tl)
                                sl = slots[j // 8]
                                pairs.append((sl.w(j % 8), rap, rkeys + [sl.key]))
                            k.mm(P[b][:, :], pairs, b)
                            ho.evac(c, tb, b)
                            for sl in slots:
                                ws.release(sl)
                        ho.finish(nxt, hf)
                    continue
                ho = HalfOut(l, n_post, True, tbs)
                for c in range(8):
                    slots = [ws.acquire(("d", l, f, c, pc)) for pc in range(3)]
                    for tb in tbs:
                        tl = ((tb % 2) * TB, (tb % 2 + 1) * TB)
                        b = k.bank()
                        pairs = []
                        for j in range(NJ):
                            rap, rkeys = H(j, tl)
                            sl = slots[j // 8]
                            pairs.append((sl.w(j % 8), rap, rkeys + [sl.key]))
                        k.mm(P[b][:, :], pairs, b)
                        ho.evac(c, tb, b)
                    for sl in slots:
                        ws.release(sl)
                ho.finish(nxt, hf)

        def even_mixer(l, n_post, nxt):
            for tb in range(NTB):
                need_xn(tb)
            idap = CB[:, CB_ID:CB_ID + 128]
            def build_dg(i):
                dg = DG[i % 2]
                for j in range(31):
                    dap, dkeys = dg(j, (0, 128))
                    col = CF[:, CF_DW + i * 31 + j:CF_DW + i * 31 + j + 1]
                    k.op("dve", lambda e, dap=dap, col=col: e.tensor_scalar(out=dap, in0=idap, scalar1=col, scalar2=None,
                                                                             op0=ALU.mult),
                         reads=CONST, writes=dkeys)
            for i in range(4):
                for (a, b_) in ((0, 15), (15 + S, S + 32)):
                    ap, keys = CC(i, (a, b_))
                    k.op("pool", lambda e, ap=ap: e.memset(ap, 0.0), writes=keys)
            for hf, i in [(h_, i_) for h_ in range(2) for i_ in range(4)]:
                wgt = ws.acquire(("lin", "ev_w_in", 0, 8 + i))
                wvl = ws.acquire(("lin", "ev_w_in", 0, 4 + i))
                for tb in (2 * hf, 2 * hf + 1):
                    lo, hi = tbr(tb)
                    bg = k.bank()
                    k.mm(P[bg][:, :], [(wgt.w(kk),) + tuple(_rk(XN(kk, (lo, hi)), wgt.key)) for kk in range(8)], bg)
                    s = k.rot("S", 2)
                    k.op("act", lambda e, bg=bg, s=s: e.activation(out=SS[s][:, :], in_=P[bg][:, :], func=AF.Sigmoid),
                         reads=[("P", bg)], writes=[("S", s)])
                    bv = k.bank()
                    k.mm(P[bv][:, :], [(wvl.w(kk),) + tuple(_rk(XN(kk, (lo, hi)), wvl.key)) for kk in range(8)], bv)
                    cap, ckeys = CC(i, (15 + lo, 15 + hi))
                    k.op("dve", lambda e, bv=bv, s=s, cap=cap: e.tensor_tensor(out=cap, in0=P[bv][:, :], in1=SS[s][:, :],
                                                                                op=ALU.mult),
                         reads=[("P", bv), ("S", s)], writes=ckeys)
                ws.release(wgt)
                ws.release(wvl)
            for hf, g in [(h_, g_) for h_ in range(2) for g_ in range(4)]:
                wf = ws.acquire(("lin", "ev_w_in", 0, g))
                for tb in (2 * hf, 2 * hf + 1):
                    lo, hi = tbr(tb)
                    b = k.bank()
                    k.mm(P[b][:, :], [(wf.w(kk),) + tuple(_rk(XN(kk, (lo, hi)), wf.key)) for kk in range(8)], b)
                    oap, okeys = UF(g, (lo, hi))
                    evac_copy(oap, okeys, b, eng=("act" if hf == 0 else None))
                ws.release(wf)
            k.dma("pool", lambda e: e.dma_start(out=ALT.h[0:1, :], in_=altd), "c1", writes=ALT((0, S))[1])
            for g in range(4):
                usy, usyk = USY(g, (1, 1024))
                uas, uask = UAS(g, (1, 1024))
                ufw, ufwk = UF(g, (1, 1024))
                ufr = UF.h[:, g, 2047:1024:-1]
                ufrk = UF(g, (1025, S))[1]
                k.op("dve", lambda e, usy=usy, ufw=ufw, ufr=ufr: e.tensor_tensor(out=usy, in0=ufw, in1=ufr, op=ALU.add),
                     reads=ufwk + ufrk, writes=usyk)
                k.op("dve", lambda e, uas=uas, ufw=ufw, ufr=ufr: e.tensor_tensor(out=uas, in0=ufw, in1=ufr, op=ALU.subtract),
                     reads=ufwk + ufrk, writes=uask)
                a0, a0k = USY(g, (0, 1))
                k.op("dve", lambda e, a0=a0, g=g: e.tensor_copy(out=a0, in_=UF.h[:, g, 0:1]), reads=UF(g, (0, 1))[1], writes=a0k)
                a1, a1k = USY(g, (1024, 1152))
                k.op("dve", lambda e, a1=a1: e.memset(a1, 0.0), writes=a1k)
                a2, a2k = USY(g, (1024, 1025))
                k.op("dve", lambda e, a2=a2, g=g: e.tensor_copy(out=a2, in_=UF.h[:, g, 1024:1025]), reads=UF(g, (1024, 1025))[1],
                     writes=a2k)
                z0, z0k = UAS(g, (0, 1))
                k.op("dve", lambda e, z0=z0: e.memset(z0, 0.0), writes=z0k)
            ccos = CB[:, CB_CSC:CB_CSC + 128]
            csin = CB[:, CB_CSC + 128:CB_CSC + 256]
            for tt in range(9):
                for g in range(4):
                    b = k.bank()
                    uap, ukeys = USY(g, (tt * 128, (tt + 1) * 128))
                    k.mm(P[b][:, 0:128], [(uap, ccos, ukeys + CONST)], b)
                    oap, okeys = AB(tt, g, (0, 128))
                    evac_copy(oap, okeys, b, n=128)
                    if tt < 8:
                        b = k.bank()
                        uap, ukeys = UAS(g, (tt * 128, (tt + 1) * 128))
                        k.mm(P[b][:, 0:128], [(uap, csin, ukeys + CONST)], b)
                        oap, okeys = AB(tt, g, (128, 256))
                        evac_copy(oap, okeys, b, n=128)
            build_dg(0)
            build_dg(1)
            for kb in range(4):
                acc = [k.bank(reserve=True) for g in range(4)]
                for ttp in range(4):
                    for which in range(2):
                        w = ws.acquire(("dft", which, kb, ttp))
                        for g in range(4):
                            pairs = []
                            for ti in range(2):
                                tt = 2 * ttp + ti
                                lap, lkeys = AB(tt, g, (which * 128, (which + 1) * 128))
                                pairs.append((lap, w.t[:, ti * TB:(ti + 1) * TB], lkeys + [w.key]))
                            k.mm(P[acc[g]][:, :], pairs, acc[g], start=(ttp == 0 and which == 0), stop=False)
                        ws.release(w)
                for g in range(4):
                    lkeys = AB(8, g, (0, 128))[1]
                    k.mm(P[acc[g]][:, :], [(AB.h[0:1, 8, g, 0:128], ALT.h[0:1, kb * TB:(kb + 1) * TB],
                                            lkeys + ALT((0, S))[1])], acc[g], start=False, stop=True)

                def step3(kb=kb, acc=acc):
                    lo, hi = tbr(kb)
                    for g in range(4):
                        fap, fkeys = FB(g, (0, TB))
                        evac_copy(fap, fkeys, acc[g])
                        k.reserved.discard(acc[g])
                    for g in range(4):
                        fap, fkeys = FB(g, (0, TB))
                        b = k.bank()
                        k.mm(P[b][:, :], [(CB[:, CB_FW + g * 128:CB_FW + (g + 1) * 128], fap, fkeys + CONST)], b)
                        oap, okeys = YA(g, (lo, hi))
                        evac_copy(oap, okeys, b)
                k.defer(6, ("f3", kb), step3)
            k.flush()
            for i in range(4):
                dg = DG[i % 2]
                for tb in range(NTB):
                    lo, hi = tbr(tb)
                    b = k.bank()
                    pairs = []
                    for j in range(31):
                        dap, dkeys = dg(j, (0, 128))
                        cap, ckeys = CC(i, (lo + j, hi + j))
                        pairs.append((dap, cap, dkeys + ckeys))
                    k.mm(P[b][:, :], pairs, b)
                    gi = k.rot("GS", 3)
                    c1, c1k = GS["c1"][gi]((0, TB))
                    dv, dvk = GS["d"][gi]((0, TB))
                    d2, d2k = GS["d2"][gi]((0, TB))
                    sd, sdk = GS["sd"][gi]((0, TB))
                    c1b, c1bk = GSB["c1b"][gi]((0, TB))
                    d2b, d2bk = GSB["d2b"][gi]((0, TB))
                    k.op("act", lambda e, b=b, c1=c1, i=i: e.activation(out=c1, in_=P[b][:, :], func=AF.Identity,
                                                                        bias=CF[:, CF_DWB + i:CF_DWB + i + 1]),
                         reads=[("P", b)] + CONST, writes=c1k)
                    k.op("act", lambda e, b=b, c1b=c1b, i=i: e.activation(out=c1b, in_=P[b][:, :], func=AF.Identity,
                                                                          bias=CF[:, CF_DWB + i:CF_DWB + i + 1]),
                         reads=[("P", b)] + CONST, writes=c1bk)

                    gkeys = c1k + dvk + d2k + sdk + YB(i, (lo, hi))[1]

                    def stage_b(i=i, lo=lo, hi=hi, c1=c1, c1k=c1k, dv=dv, dvk=dvk, d2=d2, d2k=d2k, sd=sd, sdk=sdk, gkeys=gkeys,
                                c1b=c1b, c1bk=c1bk, d2b=d2b, d2bk=d2bk):
                        b2 = k.bank()
                        k.mm(P[b2][:, :], [(ONES1[:, :], c1b, c1bk + CONST)], b2)
                        k.op("dve", lambda e: e.scalar_tensor_tensor(out=dv, in0=P[b2][:, :], scalar=-1.0 / 128.0, in1=c1,
                                                                     op0=ALU.mult, op1=ALU.add),
                             reads=c1k + [("P", b2)], writes=dvk)
                        k.op("act", lambda e: e.activation(out=d2b, in_=dv, func=AF.Square), reads=dvk, writes=d2bk)

                        def stage_c():
                            b3 = k.bank()
                            k.mm(P[b3][:, :], [(ONES1[:, :], d2b, d2bk + CONST)], b3)
                            k.op("act", lambda e: e.activation(out=sd, in_=P[b3][:, :], func=AF.Ln, bias=EPS, scale=1.0 / 128.0),
                                 reads=[("P", b3)], writes=sdk)
                            k.op("act", lambda e: e.activation(out=sd, in_=sd, func=AF.Exp, scale=-0.5), reads=sdk, writes=sdk)
                            k.op("dve", lambda e: e.scalar_tensor_tensor(
                                out=d2, in0=dv, scalar=CF[:, CF_GNG + i:CF_GNG + i + 1], in1=sd, op0=ALU.mult, op1=ALU.mult),
                                reads=dvk + sdk + CONST, writes=d2k)
                            yap, ykeys = YB(i, (lo, hi))
                            k.op("act", lambda e: e.activation(out=yap, in_=d2, func=AF.Silu,
                                                               bias=CF[:, CF_GNB + i:CF_GNB + i + 1]),
                                 reads=d2k + CONST, writes=ykeys)
                        k.defer(1, ("gln",), stage_c, gkeys, gkeys)
                    k.defer(1, ("gln",), stage_b, gkeys, gkeys)
                if i + 2 < 4:
                    build_dg(i + 2)

            def rhs(kk, tb):
                lo, hi = tbr(tb)
                return tuple_list(YA(kk, (lo, hi)) if kk < 4 else YB(kk - 4, (lo, hi)))
            out_proj(lambda c: [("lin", "ev_w_out", 0, c)], rhs, 8, l, n_post, False, nxt)

        def odd_mixer(l, n_post, nxt):
            for tb in range(NTB):
                need_xn(tb)
            for cvb in range(2):
                for (a, b_) in ((0, 1), (1 + S, S + 64)):
                    ap, keys = CV[cvb]((a, b_))
                    k.op("pool", lambda e, ap=ap: e.memset(ap, 0.0), writes=keys)

            def part_b(i, wb):
                tap, tkeys = TT[i % 2]((0, S))
                for tb in range(NTB):
                    lo, hi = tbr(tb)
                    bb = k.bank()
                    k.mm(P[bb][:, :], [(wb.w(kk),) + tuple(_rk(XN(kk, (lo, hi)), wb.key)) for kk in range(8)], bb)
                    tslice, tk = TT[i % 2]((lo, hi))
                    yap, ykeys = YO(i, (lo, hi))
                    k.op("dve", lambda e, bb=bb, tslice=tslice, yap=yap: e.tensor_tensor(out=yap, in0=P[bb][:, :], in1=tslice,
                                                                                          op=ALU.mult),
                         reads=[("P", bb)] + tk, writes=ykeys)

            pend = None
            for i in range(8):
                wc = ws.acquire(("lin", "od_w_in", 0, 8 + i))
                wv = ws.acquire(("lin", "od_w_in", 0, 16 + i))
                wb = ws.acquire(("lin", "od_w_in", 0, i))
                cv = CV[i % 2]
                for tb in range(NTB):
                    lo, hi = tbr(tb)
                    bc = k.bank()
                    k.mm(P[bc][:, :], [(wc.w(kk),) + tuple(_rk(XN(kk, (lo, hi)), wc.key)) for kk in range(8)], bc)
                    s = k.rot("S", 2)
                    k.op("act", lambda e, bc=bc, s=s: e.activation(out=SS[s][:, :], in_=P[bc][:, :], func=AF.Copy),
                         reads=[("P", bc)], writes=[("S", s)])
                    bv = k.bank()
                    k.mm(P[bv][:, :], [(wv.w(kk),) + tuple(_rk(XN(kk, (lo, hi)), wv.key)) for kk in range(8)], bv)
                    cap, ckeys = cv((1 + lo, 1 + hi))
                    k.op("dve", lambda e, bv=bv, s=s, cap=cap: e.tensor_tensor(out=cap, in0=P[bv][:, :], in1=SS[s][:, :],
                                                                                op=ALU.mult),
                         reads=[("P", bv), ("S", s)], writes=ckeys)
                tap, tkeys = TT[i % 2]((0, S))
                w0 = CF[:, CF_OD + i * 3 + 0:CF_OD + i * 3 + 1]
                w1 = CF[:, CF_OD + i * 3 + 1:CF_OD + i * 3 + 2]
                w2 = CF[:, CF_OD + i * 3 + 2:CF_OD + i * 3 + 3]
                c0, c0k = cv((0, S))
                c1_, c1k = cv((1, S + 1))
                c2, c2k = cv((2, S + 2))
                k.op("dve", lambda e, tap=tap, c0=c0, w0=w0: e.tensor_scalar(out=tap, in0=c0, scalar1=w0, scalar2=None,
                                                                             op0=ALU.mult),
                     reads=c0k + CONST, writes=tkeys)
                k.op("dve", lambda e, tap=tap, c1_=c1_, w1=w1: e.scalar_tensor_tensor(
                    out=tap, in0=c1_, scalar=w1, in1=tap, op0=ALU.mult, op1=ALU.add), reads=c1k + tkeys + CONST, writes=tkeys)
                k.op("dve", lambda e, tap=tap, c2=c2, w2=w2: e.scalar_tensor_tensor(
                    out=tap, in0=c2, scalar=w2, in1=tap, op0=ALU.mult, op1=ALU.add), reads=c2k + tkeys + CONST, writes=tkeys)
                if pend is not None:
                    part_b(*pend[:2])
                    for sl in pend[2]:
                        ws.release(sl)
                pend = (i, wb, (wc, wv, wb))
            part_b(*pend[:2])
            for sl in pend[2]:
                ws.release(sl)

            def rhs(kk, tb):
                lo, hi = tbr(tb)
                return tuple_list(YO(kk, (lo, hi)))
            out_proj(lambda c: [("lin", "od_w_out", 0, c)], rhs, 8, l, n_post, False, nxt)

        def attention(l, n_post, nxt):
            for tb in range(NTB):
                need_xn(tb)
            m32, m32k = M32((0, 8), (0, MEM))
            k.dma("sp", lambda e: e.dma_start(out=m32, in_=memT.rearrange("(c p) m -> p c m", p=128)), "ml", writes=m32k)
            mst = {"pend": [], "mb": None, "gi": 0}

            def mem_sq(c):
                map_, mk = M32(c, (0, MEM))
                i = k.rot("SQD", 3)
                k.op("act", lambda e: e.activation(out=SQD[i][:, 0:MEM], in_=map_, func=AF.Square),
                     reads=mk, writes=[("SQD", i)])
                mst["pend"].append((c, i))

            def mem_mm():
                c, i = mst["pend"].pop(0)
                if mst["mb"] is None:
                    mst["mb"] = k.bank(reserve=True)
                mb = mst["mb"]
                k.mm(P[mb][:, 0:MEM], [(ONESM[:, :], SQD[i][:, 0:MEM], [("SQD", i), ("ONES",)])], mb,
                     start=(c == 0), stop=(c == 7))

            def mem_apply():
                r = rstd_from(mst["mb"], MEM, 1.0)
                for c in range(8):
                    map_, mk = M32(c, (0, MEM))
                    oap, okeys = MN(c, (0, MEM))
                    col = CF[:, CF_MG + l * 8 + c:CF_MG + l * 8 + c + 1]
                    k.op("dve", lambda e, map_=map_, oap=oap, col=col: e.scalar_tensor_tensor(
                        out=oap, in0=map_, scalar=col, in1=R[r][:, 0:MEM], op0=ALU.mult, op1=ALU.mult),
                        reads=mk + [("R", r)] + CONST, writes=okeys)
            for hf, c8 in [(0, c_) for c_ in range(8)]:
                w = ws.acquire(("lin", "xa_wq", l, c8))
                for tb in (2 * hf, 2 * hf + 1):
                    lo, hi = tbr(tb)
                    b = k.bank()
                    k.mm(P[b][:, :], [(w.w(kk),) + tuple(_rk(XN(kk, (lo, hi)), w.key)) for kk in range(8)], b)
                    oap, okeys = Q(c8, (lo, hi))
                    evac_copy(oap, okeys, b, eng="act")
                    gi = mst["gi"]
                    mst["gi"] += 1
                    if 6 <= gi < 14:
                        mem_mm()
                    if 4 <= gi < 12:
                        mem_sq(gi - 4)
                    if gi == 14:
                        mem_apply()
                ws.release(w)
            for hf, c8 in [(1, c_) for c_ in range(4)]:
                w = ws.acquire(("lin", "xa_wq", l, c8))
                for tb in (2 * hf, 2 * hf + 1):
                    lo, hi = tbr(tb)
                    b = k.bank()
                    k.mm(P[b][:, :], [(w.w(kk),) + tuple(_rk(XN(kk, (lo, hi)), w.key)) for kk in range(8)], b)
                    oap, okeys = Q(c8, (lo, hi))
                    evac_copy(oap, okeys, b)
                ws.release(w)
            for c8 in range(8):
                w = ws.acquire(("lin", "xa_wkv", l, c8))
                b = k.bank()
                k.mm(P[b][:, 0:MEM], [(w.w(kk),) + tuple(_rk(MN(kk, (0, MEM)), w.key)) for kk in range(8)], b)
                oap, okeys = KT(c8, (0, MEM))
                evac_copy(oap, okeys, b, n=MEM, eng="act")
                ws.release(w)
            for cb in range(4):
                wv0 = ws.acquire(("wv", l, cb, 0))
                wv1 = ws.acquire(("wv", l, cb, 1))
                for mt in range(2):
                    b = k.bank()
                    pairs = []
                    for kk in range(8):
                        map_, mk = MN(kk, (mt * 128, (mt + 1) * 128))
                        sl = wv0 if kk < 4 else wv1
                        pairs.append((map_, sl.t[:, (kk % 4) * 256:(kk % 4 + 1) * 256], mk + [sl.key]))
                    k.mm(P[b][:, 0:256], pairs, b)
                    oap, okeys = V(mt, (cb * 256, (cb + 1) * 256))
                    evac_copy(oap, okeys, b, n=256)
                ws.release(wv0)
                ws.release(wv1)
            for hf, c8 in [(1, c_) for c_ in range(4, 8)]:
                w = ws.acquire(("lin", "xa_wq", l, c8))
                for tb in (2 * hf, 2 * hf + 1):
                    lo, hi = tbr(tb)
                    b = k.bank()
                    k.mm(P[b][:, :], [(w.w(kk),) + tuple(_rk(XN(kk, (lo, hi)), w.key)) for kk in range(8)], b)
                    oap, okeys = Q(c8, (lo, hi))
                    evac_copy(oap, okeys, b)
                ws.release(w)
            for sb_ in range(NTB):
                lo, hi = tbr(sb_)
                for h in range(4):
                    et = ET[k.rot("ET", 3)]
                    for mt in range(2):
                        b = k.bank()
                        pairs = []
                        for dc in range(2):
                            kap, kk_ = KT(2 * h + dc, (mt * 128, (mt + 1) * 128))
                            qap, qk = Q(2 * h + dc, (lo, hi))
                            pairs.append((kap, qap, kk_ + qk))
                        k.mm(P[b][:, :], pairs, b)
                        eap, ek = et(mt, (0, TB))
                        k.op("act", lambda e, b=b, eap=eap: e.activation(out=eap, in_=P[b][:, :], func=AF.Exp, scale=1.0 / 16.0),
                             reads=[("P", b)], writes=ek)

                    def stage2(h=h, et=et, lo=lo, hi=hi):
                        b = k.bank()
                        pairs = []
                        for mt in range(2):
                            eap, ek = et(mt, (0, TB))
                            pairs.append((ONES1[:, :], eap, ek + CONST))
                        k.mm(P[b][:, :], pairs, b)
                        rd = RD[k.rot("RD", 2)]
                        rap, rk = rd((0, TB))
                        k.op("act", lambda e: e.activation(out=rap, in_=P[b][:, :], func=AF.Ln), reads=[("P", b)], writes=rk)
                        k.op("act", lambda e: e.activation(out=rap, in_=rap, func=AF.Exp, scale=-1.0), reads=rk, writes=rk)
                        for dc in range(2):
                            b2 = k.bank()
                            pairs = []
                            for mt in range(2):
                                vap, vk = V(mt, (h * 256 + dc * 128, h * 256 + (dc + 1) * 128))
                                eap, ek = et(mt, (0, TB))
                                pairs.append((vap, eap, vk + ek))
                            k.mm(P[b2][:, :], pairs, b2)
                            oap, okeys = O(2 * h + dc, (lo, hi))
                            k.op("dve", lambda e, b2=b2, oap=oap: e.tensor_tensor(out=oap, in0=P[b2][:, :], in1=rap, op=ALU.mult),
                                 reads=[("P", b2)] + rk, writes=okeys)
                    akeys = et(0, (0, TB))[1] + et(1, (0, TB))[1] + RD[0]((0, TB))[1] + RD[1]((0, TB))[1] \
                        + O(2 * h, (lo, hi))[1] + O(2 * h + 1, (lo, hi))[1]
                    k.defer(2, ("att",), stage2, akeys, akeys)

            def rhs(kk, tb):
                lo, hi = tbr(tb)
                return tuple_list(O(kk, (lo, hi)))
            out_proj(lambda c: [("lin", "xa_wo", l, c)], rhs, 8, l, n_post, False, nxt, ho2=lambda c, tl: HO2(c, tl))

        subs = []
        for l in range(L):
            subs += [("ffn", l, 0, 0, 1), ("mix", l, None, 2, 3), ("att", l, None, 4, 5), ("ffn", l, 1, 6, 7)]
        subs = subs[:nsub]
        prenorm_chain(0, 0, list(range(NTB)))
        for si, (kind, l, f, n_pre, n_post) in enumerate(subs):
            nxt = (subs[si + 1][1], subs[si + 1][3]) if si + 1 < len(subs) else None
            if kind == "ffn":
                ffn(l, f, n_post, nxt)
            elif kind == "mix":
                (even_mixer if l % 2 == 0 else odd_mixer)(l, n_post, nxt)
            else:
                attention(l, n_post, nxt)
        k.flush()
        assert k.cnt.get("yo", 0) == 16 * NTB
        k.wait_all("sp", ["yo"])

    def _rk(apk, key):
        return (apk[0], apk[1] + [key])

    def tuple_list(apk):
        return (apk[0], list(apk[1]))

    kd = K(None)
    program(kd)
    sched = list(kd.ws.log)
    k = K(sched)
    program(k)

    semh = {name: nc.alloc_semaphore("s_" + name) for name in k.cnt}

    def replay(eng_name, e):
        for waits, fn, inc in k.q[eng_name]:
            for s_, v in waits:
                e.wait_ge(semh[s_], v)
            if fn is None:
                continue
            ins = fn(e)
            if inc is not None:
                ins.then_inc(semh[inc[0]], inc[1])

    with nc.Block() as block:
        @block.sync
        def _(e):
            replay("sp", e)

        @block.gpsimd
        def _(e):
            replay("pool", e)

        @block.scalar
        def _(e):
            replay("act", e)

        @block.vector
        def _(e):
            replay("dve", e)

        @block.tensor
        def _(e):
            replay("pe", e)
    return nc


_NC_CACHE = {}


def run(inputs, nsub=8, cores=NCORES, trace=False):
    inp = {k_: np.asarray(v, dtype=np.float32) for k_, v in inputs.items()}
    wflat = pack_weights(inp)
    cf, cb = pack_consts(inp)
    alt = (((-1.0) ** np.arange(S)) / np.sqrt(float(S))).astype(np.float32).reshape(1, S)
    if nsub not in _NC_CACHE:
        _NC_CACHE[nsub] = build(nsub)
    nc = _NC_CACHE[nsub]
    in_maps = []
    for b in range(cores):
        in_maps.append({
            "xT": np.ascontiguousarray(inp["x"][b].T),
            "memT": np.ascontiguousarray(inp["mem"][b].T),
            "wflat": wflat, "cf": cf, "cb": cb, "alt": alt,
        })
    res = run_bass_kernel_spmd(nc, in_maps, core_ids=list(range(cores)), trace=trace)
    out = np.stack([np.ascontiguousarray(res.results[b]["yT"].T) for b in range(cores)], axis=0)
    return out.astype(np.float32), res


def kernel(**inputs):
    out, _ = run(inputs)
    return out
```
